# Optimizing a Trainium2 kernel written in Bass

```python
import math
import jax, jax.numpy as jnp
from jax import lax
import numpy as np

D_MODEL = 1024
BATCH = 16
SEQ = 2048
DEPTH = 2
DEC_BATCH = 16
DEC_SEQ = 4096
PAST_LEN = 128

GRID_W = 64
BLOCK = 128
N_BRANCH = 4
BRANCH_W = 512
CONV_K = 31
DIFF_HEADS = 4
DIFF_DH = 64
DIFF_VD = 2 * DIFF_DH
WIN = 128
WIN_HEADS = 8
WIN_KV = 2
WIN_DH = 64
AX_HEADS = 8
AX_KV = 2
AX_DH = 64
ROPE_THETA = 10000.0
MEM_LEN = 256
X_HEADS = 4
X_DH = D_MODEL // X_HEADS
FFN_DIM = 2752
FFN_CONV_K = 3
NUM_BUCKETS = 32
MAX_DISTANCE = 128
N_BIAS = 2 * DIFF_HEADS + WIN_HEADS
LN_EPS = 1e-5
NEG = -1e30
DN_ALPHA = (2 * DEPTH) ** 0.25
DN_BETA = (8 * DEPTH) ** -0.25
SPLITS = (2 * BRANCH_W,
          DIFF_HEADS * 2 * DIFF_DH, DIFF_HEADS * 2 * DIFF_DH, DIFF_HEADS * DIFF_VD,
          WIN_HEADS * WIN_DH, WIN_KV * WIN_DH, WIN_KV * WIN_DH,
          AX_HEADS * AX_DH, AX_KV * AX_DH, AX_KV * AX_DH,
          N_BRANCH * D_MODEL)
N_IN = sum(SPLITS)

kernel_name = 'hybrid_bidir_encoder_two_groups'


def layer_norm(x, g, b):
    xf = x.astype(jnp.float32)
    mu = jnp.mean(xf, -1, keepdims=True)
    var = jnp.mean(jnp.square(xf - mu), -1, keepdims=True)
    y = (xf - mu) * lax.rsqrt(var + LN_EPS) * g.astype(jnp.float32) + b.astype(jnp.float32)
    return y.astype(x.dtype)


def rms_norm(x, g):
    xf = x.astype(jnp.float32)
    y = xf * lax.rsqrt(jnp.mean(xf * xf, -1, keepdims=True) + LN_EPS) * g.astype(jnp.float32)
    return y.astype(x.dtype)


def depthwise_conv(x, w, b):
    k = w.shape[0]
    y = lax.conv_general_dilated(x, w[:, None, :].astype(x.dtype), window_strides=(1,),
                                 padding=[(k // 2, k // 2)],
                                 dimension_numbers=('NWC', 'WIO', 'NWC'),
                                 feature_group_count=x.shape[-1])
    return y + b


def t5_bucket(rel):
    half = NUM_BUCKETS // 2
    max_exact = half // 2
    n = jnp.abs(rel)
    nf = jnp.maximum(n, 1).astype(jnp.float32)
    large = max_exact + (jnp.log(nf / max_exact) / math.log(MAX_DISTANCE / max_exact)
                         * (half - max_exact)).astype(jnp.int32)
    large = jnp.minimum(large, half - 1)
    return jnp.where(rel > 0, half, 0) + jnp.where(n < max_exact, n, large)


def to_blocks(x):
    b, s = x.shape[:2]
    return jnp.moveaxis(x.reshape((b, s // BLOCK, BLOCK) + x.shape[2:]), 1, 0)


def from_blocks(y):
    nb, b = y.shape[:2]
    y = jnp.moveaxis(y, 0, 1)
    return y.reshape((b, nb * y.shape[2]) + y.shape[3:])


def conv_module(u, conv_w, conv_b, ln_g, ln_b):
    h = u[..., :BRANCH_W] * jax.nn.sigmoid(u[..., BRANCH_W:])
    h = depthwise_conv(h, conv_w, conv_b)
    h = layer_norm(h, ln_g, ln_b)
    return jax.nn.silu(h)


def diff_attention(q, k, v, lam, sub_g, bias_table, lam_init):
    b, s = q.shape[:2]
    lf = lam.astype(jnp.float32)
    lmb = jnp.exp(jnp.sum(lf[0] * lf[1])) - jnp.exp(jnp.sum(lf[2] * lf[3])) + lam_init
    scale = DIFF_DH ** -0.5
    kpos = jnp.arange(s)

    def block(args):
        i, qb = args
        qpos = i * BLOCK + jnp.arange(BLOCK)
        bias = bias_table[t5_bucket(kpos[None, :] - qpos[:, None])]
        bias = jnp.transpose(bias, (2, 3, 0, 1)).astype(jnp.float32)
        sc = jnp.einsum('bqhmd,bkhmd->bhmqk', qb, k).astype(jnp.float32) * scale + bias
        p = jax.nn.softmax(sc, axis=-1)
        a = p[:, :, 0] - lmb * p[:, :, 1]
        return jnp.einsum('bhqk,bkhe->bqhe', a.astype(v.dtype), v)

    o = from_blocks(lax.map(block, (jnp.arange(s // BLOCK), to_blocks(q))))
    o = rms_norm(o, sub_g) * (1.0 - lam_init)
    return o.reshape(b, s, DIFF_HEADS * DIFF_VD)


def window_attention(q, k, v, sink, bias_table):
    b, s = q.shape[:2]
    span = BLOCK + 2 * WIN
    scale = WIN_DH ** -0.5
    kp = jnp.pad(k, ((0, 0), (WIN, WIN), (0, 0), (0, 0)))
    vp = jnp.pad(v, ((0, 0), (WIN, WIN), (0, 0), (0, 0)))
    off = jnp.arange(span)[None, :] - WIN - jnp.arange(BLOCK)[:, None]
    bias = jnp.transpose(bias_table[t5_bucket(off)], (2, 3, 0, 1)).astype(jnp.float32)
    in_win = jnp.abs(off) <= WIN
    sink_f = sink.astype(jnp.float32)[None, :, :, None, None]

    def block(args):
        i, qb = args
        kb = lax.dynamic_slice_in_dim(kp, i * BLOCK, span, axis=1)
        vb = lax.dynamic_slice_in_dim(vp, i * BLOCK, span, axis=1)
        kpos = i * BLOCK - WIN + jnp.arange(span)
        mask = in_win & ((kpos >= 0) & (kpos < s))[None, :]
        sc = jnp.einsum('bqhgd,bkhd->bhgqk', qb, kb).astype(jnp.float32) * scale + bias
        sc = jnp.where(mask, sc, NEG)
        m = jnp.maximum(jnp.max(sc, -1, keepdims=True), sink_f)
        p = jnp.exp(sc - m)
        p = p / (jnp.sum(p, -1, keepdims=True) + jnp.exp(sink_f - m))
        return jnp.einsum('bhgqk,bkhd->bqhgd', p.astype(v.dtype), vb)

    o = from_blocks(lax.map(block, (jnp.arange(s // BLOCK), to_blocks(q))))
    return o.reshape(b, s, WIN_HEADS * WIN_DH)


def axial_angles(s):
    rows = s // GRID_W
    row = jnp.repeat(jnp.arange(rows, dtype=jnp.float32), GRID_W)
    col = jnp.tile(jnp.arange(GRID_W, dtype=jnp.float32), rows)
    n_freq = AX_DH // 4
    inv = ROPE_THETA ** (-jnp.arange(n_freq, dtype=jnp.float32) / n_freq)
    return row[:, None] * inv, col[:, None] * inv


def rot_half(x, ang):
    n = x.shape[-1] // 2
    c = jnp.cos(ang)[None, :, None, :]
    sn = jnp.sin(ang)[None, :, None, :]
    x1, x2 = x[..., :n], x[..., n:]
    return jnp.concatenate([x1 * c - x2 * sn, x2 * c + x1 * sn], -1)


def axial_rope(x, ang_r, ang_c):
    xf = x.astype(jnp.float32)
    h = AX_DH // 2
    return jnp.concatenate([rot_half(xf[..., :h], ang_r), rot_half(xf[..., h:], ang_c)], -1).astype(x.dtype)


def axial_attention(q, k, v, qn_g, kn_g):
    b, s = q.shape[:2]
    scale = AX_DH ** -0.5
    ang_r, ang_c = axial_angles(s)
    q = axial_rope(rms_norm(q, qn_g), ang_r, ang_c).reshape(b, s, AX_KV, AX_HEADS // AX_KV, AX_DH)
    k = axial_rope(rms_norm(k, kn_g), ang_r, ang_c)

    def block(qb):
        sc = jnp.einsum('bqhgd,bkhd->bhgqk', qb, k).astype(jnp.float32) * scale
        p = jax.nn.softmax(sc, axis=-1)
        return jnp.einsum('bhgqk,bkhd->bqhgd', p.astype(v.dtype), v)

    o = from_blocks(lax.map(block, to_blocks(q)))
    return o.reshape(b, s, AX_HEADS * AX_DH)


def token_mixing(x, layer, w_in, b_in, conv_w, conv_b, cln_g, cln_b, lam, sub_g, sink,
                 qn_g, kn_g, w_branch, w_out, diff_bias, win_bias):
    b, s, _ = x.shape
    u = x @ w_in + b_in
    offs = np.cumsum(SPLITS)[:-1].tolist()
    ua, bq, bk, bv, cq, ck, cv, dq, dk, dv, gl = jnp.split(u, offs, axis=-1)
    lam_init = 0.8 - 0.6 * math.exp(-0.3 * layer)
    o_a = conv_module(ua, conv_w, conv_b, cln_g, cln_b)
    o_b = diff_attention(bq.reshape(b, s, DIFF_HEADS, 2, DIFF_DH), bk.reshape(b, s, DIFF_HEADS, 2, DIFF_DH),
                         bv.reshape(b, s, DIFF_HEADS, DIFF_VD), lam, sub_g, diff_bias, lam_init)
    o_c = window_attention(cq.reshape(b, s, WIN_KV, WIN_HEADS // WIN_KV, WIN_DH),
                           ck.reshape(b, s, WIN_KV, WIN_DH), cv.reshape(b, s, WIN_KV, WIN_DH),
                           sink.reshape(WIN_KV, WIN_HEADS // WIN_KV), win_bias)
    o_d = axial_attention(dq.reshape(b, s, AX_HEADS, AX_DH), dk.reshape(b, s, AX_KV, AX_DH),
                          dv.reshape(b, s, AX_KV, AX_DH), qn_g, kn_g)
    gates = jax.nn.sigmoid(gl.reshape(b, s, N_BRANCH, D_MODEL))
    merged = gates[:, :, 0] * (o_a @ w_branch[0])
    for n, o in ((1, o_b), (2, o_c), (3, o_d)):
        merged = merged + gates[:, :, n] * (o @ w_branch[n])
    return merged @ w_out


def memory_cross_attention(x, mem, w_q, w_kv, w_o):
    b, s, _ = x.shape
    m = mem.shape[1]
    q = (x @ w_q).reshape(b, s, X_HEADS, X_DH)
    kv = (mem @ w_kv).reshape(b, m, 2, X_HEADS, X_DH)
    sc = jnp.einsum('bqhd,bkhd->bhqk', q, kv[:, :, 0]).astype(jnp.float32) * (X_DH ** -0.5)
    p = jax.nn.softmax(sc, axis=-1)
    o = jnp.einsum('bhqk,bkhd->bqhd', p.astype(x.dtype), kv[:, :, 1]).reshape(b, s, D_MODEL)
    return o @ w_o


def conv_ffn(x, w_up, conv_w, conv_b, w_down):
    h = depthwise_conv(x @ w_up, conv_w, conv_b)
    g, u = h[..., :FFN_DIM], h[..., FFN_DIM:]
    return (jax.nn.gelu(g, approximate=False) * u) @ w_down


def setup_inputs(seed: int = 0) -> dict:
    key = jax.random.key(seed)
    ks = jax.random.split(key, 32)
    f32 = jnp.float32
    L = DEPTH
    D = D_MODEL

    def nrm(i, shape, scale):
        return jax.random.normal(ks[i], shape, f32) * scale

    def gain(i, shape):
        return 1.0 + nrm(i, shape, 0.02)

    return {
        'x_prompt': nrm(0, (BATCH, SEQ, D), 1.0),
        'x_sample': nrm(1, (DEC_BATCH, DEC_SEQ, D), 1.0),
        'mem_prompt': nrm(2, (BATCH, MEM_LEN, D), 1.0),
        'mem_sample': nrm(3, (DEC_BATCH, MEM_LEN, D), 1.0),
        'rel_bias': nrm(4, (NUM_BUCKETS, N_BIAS), 0.1),
        'w_in': nrm(5, (L, D, N_IN), D ** -0.5),
        'b_in': nrm(6, (L, N_IN), 0.02),
        'a_conv_w': nrm(7, (L, CONV_K, BRANCH_W), CONV_K ** -0.5),
        'a_conv_b': nrm(8, (L, BRANCH_W), 0.02),
        'a_ln_g': gain(9, (L, BRANCH_W)),
        'a_ln_b': nrm(10, (L, BRANCH_W), 0.02),
        'diff_lam': nrm(11, (L, 4, DIFF_DH), 0.1),
        'diff_sub_g': gain(12, (L, DIFF_VD)),
        'win_sink': nrm(13, (L, WIN_HEADS), 0.5),
        'ax_qn_g': gain(14, (L, AX_DH)),
        'ax_kn_g': gain(15, (L, AX_DH)),
        'w_branch': nrm(16, (L, N_BRANCH, BRANCH_W, D), BRANCH_W ** -0.5),
        'w_mix_out': nrm(17, (L, D, D), D ** -0.5 * DN_BETA),
        'ln1_g': gain(18, (L, D)),
        'ln1_b': nrm(19, (L, D), 0.02),
        'w_xq': nrm(20, (L, D, D), D ** -0.5),
        'w_xkv': nrm(21, (L, D, 2 * D), D ** -0.5),
        'w_xo': nrm(22, (L, D, D), D ** -0.5 * DN_BETA),
        'ln2_g': gain(23, (L, D)),
        'ln2_b': nrm(24, (L, D), 0.02),
        'w_up': nrm(25, (L, D, 2 * FFN_DIM), D ** -0.5),
        'f_conv_w': nrm(26, (L, FFN_CONV_K, 2 * FFN_DIM), FFN_CONV_K ** -0.5),
        'f_conv_b': nrm(27, (L, 2 * FFN_DIM), 0.02),
        'w_down': nrm(28, (L, FFN_DIM, D), FFN_DIM ** -0.5 * DN_BETA),
        'ln3_g': gain(29, (L, D)),
        'ln3_b': nrm(30, (L, D), 0.02),
    }


def reference(x_prompt, x_sample, mem_prompt, mem_sample, rel_bias, w_in, b_in, a_conv_w, a_conv_b,
              a_ln_g, a_ln_b, diff_lam, diff_sub_g, win_sink, ax_qn_g, ax_kn_g, w_branch, w_mix_out,
              ln1_g, ln1_b, w_xq, w_xkv, w_xo, ln2_g, ln2_b, w_up, f_conv_w, f_conv_b, w_down,
              ln3_g, ln3_b):
    diff_bias = rel_bias[:, :2 * DIFF_HEADS].reshape(NUM_BUCKETS, DIFF_HEADS, 2)
    win_bias = rel_bias[:, 2 * DIFF_HEADS:].reshape(NUM_BUCKETS, WIN_KV, WIN_HEADS // WIN_KV)

    def encode(x, mem):
        for l in range(DEPTH):
            h = token_mixing(x, l, w_in[l], b_in[l], a_conv_w[l], a_conv_b[l], a_ln_g[l], a_ln_b[l],
                             diff_lam[l], diff_sub_g[l], win_sink[l], ax_qn_g[l], ax_kn_g[l],
                             w_branch[l], w_mix_out[l], diff_bias, win_bias)
            x = layer_norm(DN_ALPHA * x + h, ln1_g[l], ln1_b[l])
            h = memory_cross_attention(x, mem, w_xq[l], w_xkv[l], w_xo[l])
            x = layer_norm(DN_ALPHA * x + h, ln2_g[l], ln2_b[l])
            h = conv_ffn(x, w_up[l], f_conv_w[l], f_conv_b[l], w_down[l])
            x = layer_norm(DN_ALPHA * x + h, ln3_g[l], ln3_b[l])
        return x

    y_prompt = encode(x_prompt, mem_prompt)
    y_sample = encode(x_sample, mem_sample)
    return (y_prompt, y_sample)
```

```python
import math
from contextlib import ExitStack
import numpy as np
import concourse.bass as bass
import concourse.mybir as mybir
from concourse.bass_utils import run_bass_kernel_spmd

F32 = mybir.dt.float32
BF16 = mybir.dt.bfloat16
AF = mybir.ActivationFunctionType
ALU = mybir.AluOpType
AX = mybir.AxisListType

D = 1024
DEPTH = 2
N_IN = 8192
FFN = 2752
MEM = 256
EPS = 1e-5
ALPHA = (2 * DEPTH) ** 0.25
NEGM = -30000.0
O_UA, O_BQ, O_BK, O_BV, O_CQ, O_CK, O_CV, O_DQ, O_DK, O_DV, O_GL = 0, 1024, 1536, 2048, 2560, 3072, 3200, 3328, 3840, 3968, 4096
C_BIN, C_CW, C_CB, C_ALG, C_ALB = 0, 64, 188, 192, 196
C_L1G, C_L1B, C_L2G, C_L2B, C_L3G, C_L3B = 200, 208, 216, 224, 232, 240
C_FW, C_FB, C_SUBG, C_QNG, C_KNG = 248, 380, 424, 425, 426
NCOL = 432
TD_L = 1280
TD_W = 1152
TW_L = 512
TW_W = 384


class Buf:
    __slots__ = ("w", "r")

    def __init__(self):
        self.w = {}
        self.r = {}


class Tl:
    __slots__ = ("ap", "b")

    def __init__(self, ap, b=None):
        self.ap = ap
        self.b = b if b is not None else Buf()


class Sched:
    ENG = ("pe", "act", "dve", "pool", "sp")

    def __init__(self, nc, es):
        self.nc = nc
        self.sem = {}
        self.cnt = {}
        self.seen = {e: {} for e in self.ENG}
        self.ops = {e: [] for e in self.ENG}
        for e in ("pe", "act", "dve", "pool"):
            self.sem[e] = es.enter_context(nc.semaphore("s_" + e))
            self.cnt[e] = 0
        self.dsem = {}
        self.dval = {}
        self.dnext = {}
        for q, n in (("sp", 24), ("pool", 16), ("act", 8)):
            self.dsem[q] = [es.enter_context(nc.semaphore("d_%s%d" % (q, i))) for i in range(n)]
            self.dval[q] = [0] * n
            self.dnext[q] = 0
        self.allsems = {}
        for e in ("pe", "act", "dve", "pool"):
            self.allsems[id(self.sem[e])] = self.sem[e]
        for q in self.dsem:
            for s in self.dsem[q]:
                self.allsems[id(s)] = s
        self.latest = {}
        self.n_ops = 0

    def _deps(self, eng, reads, writes, own):
        deps = {}
        for b in reads:
            for k, v in b.w.items():
                if k == own and eng == "pe":
                    continue
                if deps.get(k, 0) < v:
                    deps[k] = v
        for b in writes:
            for src in (b.w, b.r):
                for k, v in src.items():
                    if k == own:
                        continue
                    if deps.get(k, 0) < v:
                        deps[k] = v
        return deps

    def _commit(self, eng, deps, fn, key, val, inc, reads, writes):
        seen = self.seen[eng]
        waits = []
        for k, v in deps.items():
            if seen.get(k, 0) < v:
                seen[k] = v
                waits.append((self.allsems[k], v))
        self.ops[eng].append((waits, fn, self.allsems[key], inc))
        self.latest[key] = val
        for b in writes:
            b.w = {key: val}
            b.r = {}
        for b in reads:
            if b.r.get(key, 0) < val:
                b.r[key] = val
        self.n_ops += 1

    def op(self, eng, fn, r=(), w=()):
        reads = [t.b for t in r]
        writes = [t.b for t in w]
        key = id(self.sem[eng])
        deps = self._deps(eng, reads, writes, key)
        self.cnt[eng] += 1
        self._commit(eng, deps, fn, key, self.cnt[eng], 1, reads, writes)

    def dma(self, q, out, in_, r=(), w=()):
        reads = [t.b for t in r]
        writes = [t.b for t in w]
        i = self.dnext[q]
        self.dnext[q] = (i + 1) % len(self.dsem[q])
        s = self.dsem[q][i]
        key = id(s)
        deps = self._deps(q, reads, writes, None)
        prev = self.dval[q][i]
        if prev > 0 and deps.get(key, 0) < prev:
            deps[key] = prev
        self.dval[q][i] = prev + 16
        self._commit(q, deps, lambda e, o=out, a=in_: e.dma_start(out=o, in_=a), key, prev + 16, 16, reads, writes)

    def barrier(self):
        for e in self.ENG:
            seen = self.seen[e]
            waits = []
            for k, v in self.latest.items():
                if seen.get(k, 0) < v:
                    seen[k] = v
                    waits.append((self.allsems[k], v))
            if waits:
                self.ops[e].append((waits, None, None, 0))

    def emit(self):
        nc = self.nc

        def run(e, lst):
            for waits, fn, s, inc in lst:
                for ws, wv in waits:
                    e.wait_ge(ws, wv)
                if fn is not None:
                    fn(e).then_inc(s, inc)

        with nc.Block() as block:
            @block.tensor
            def _(e):
                run(e, self.ops["pe"])

            @block.scalar
            def _(e):
                run(e, self.ops["act"])

            @block.vector
            def _(e):
                run(e, self.ops["dve"])

            @block.gpsimd
            def _(e):
                run(e, self.ops["pool"])

            @block.sync
            def _(e):
                run(e, self.ops["sp"])


class Arena:
    def __init__(self, ap, dtype_bytes):
        self.ap = ap
        self.n = ap.shape[1]
        self.off = 0

    def reset(self):
        self.off = 0

    def take(self, shape):
        n = 1
        for s in shape[1:]:
            n *= s
        assert self.off + n <= self.n, ("arena overflow", self.off, n, self.n)
        v = self.ap[0:shape[0], self.off:self.off + n]
        self.off += n
        if len(shape) == 3:
            v = v.rearrange("p (a b) -> p a b", a=shape[1])
        elif len(shape) == 4:
            v = v.rearrange("p (a b c) -> p a b c", a=shape[1], b=shape[2])
        return Tl(v)


def t5_bucket_np(rel):
    half, max_exact = 16, 8
    n = np.abs(rel)
    nf = np.maximum(n, 1).astype(np.float32)
    large = max_exact + (np.log(nf / np.float32(max_exact)) / np.float32(math.log(128 / max_exact)) * np.float32(half - max_exact)).astype(np.int32)
    large = np.minimum(large, half - 1)
    return np.where(rel > 0, half, 0) + np.where(n < max_exact, n, large)


def build(seqs, debug=False):
    TOK = sum(seqs)
    NSEQ = len(seqs)
    SMAX = max(seqs)
    goff = [sum(seqs[:i]) for i in range(NSEQ)]
    tiles = []
    for s, S in enumerate(seqs):
        for t in range(S // 512):
            tiles.append((s, t * 512, goff[s] + t * 512))
    NT = len(tiles)

    nc = bass.Bass("TRN2", target_bir_lowering=False)

    def din(name, shape, dt=F32):
        return nc.dram_tensor(name, list(shape), dt, kind="ExternalInput").ap()

    def dscr(name, shape, dt):
        kind = "ExternalOutput" if debug else "Internal"
        return Tl(nc.dram_tensor(name, list(shape), dt, kind=kind).ap())

    xT = Tl(din("xT", [D, TOK]))
    memT = din("memT", [D, NSEQ * MEM])
    rel_bias = din("rel_bias", [32, 16])
    w_in = din("w_in", [DEPTH, D, N_IN])
    w_branch = din("w_branch", [DEPTH, 4, 512, D])
    w_mix_out = din("w_mix_out", [DEPTH, D, D])
    w_xq = din("w_xq", [DEPTH, D, D])
    w_xkv = din("w_xkv", [DEPTH, D, 2 * D])
    w_xo = din("w_xo", [DEPTH, D, D])
    w_up = din("w_up", [DEPTH, D, 2 * FFN])
    w_down = din("w_down", [DEPTH, FFN, D])
    ptab_d = din("ptab", [DEPTH, 128, NCOL])
    brow_d = din("brow", [DEPTH, 128, 768])
    lam_d = din("lam", [DEPTH, 1, 256])
    sink_d = din("sink", [DEPTH, 128, 8])
    cmat_d = din("cmat", [128, 512])
    rope_d = din("rope", [2, 128, SMAX])
    ohd_d = din("ohd", [33, TD_L])
    ohw_d = din("ohw", [33, TW_L])
    yT = Tl(nc.dram_tensor("yT", [D, TOK], F32, kind="ExternalOutput").ap())

    QD = dscr("QD", [512, TOK], BF16)
    KD = dscr("KD", [8, 128, TOK], BF16)
    VD = dscr("VD", [TOK, 512], BF16)
    QW = dscr("QW", [512, TOK], BF16)
    KW = dscr("KW", [2, 2, 128, TOK], BF16)
    VW = dscr("VW", [TOK, 2, 128], BF16)
    QA = dscr("QA", [512, TOK], BF16)
    KA = dscr("KA", [2, 2, 128, TOK], BF16)
    VA = dscr("VA", [TOK, 2, 128], BF16)
    OBR = [dscr("O%d" % n, [512, TOK], BF16) for n in range(4)]
    MG = dscr("MG", [D, TOK], BF16)
    ACS = dscr("ACS", [22 * 128, TOK], BF16)
    X2 = dscr("X2", [D, TOK], F32)
    X3 = dscr("X3", [D, TOK], F32)
    MK = dscr("MK", [NSEQ, D, MEM], BF16)
    MV = dscr("MV", [NSEQ, MEM, D], BF16)
    TSD = dscr("TSD", [16, TD_L], F32)
    TSW = dscr("TSW", [16, TW_L], F32)
    BTD = dscr("BTD", [8, 128, TD_W], F32)
    BTW = dscr("BTW", [8, 128, TW_W], F32)

    es = ExitStack()
    with es:
        S = Sched(nc, es)

        def sb(name, shape, dt):
            return es.enter_context(nc.sbuf_tensor("sb_" + name, list(shape), dt))

        WA_t = sb("WA", [128, 24576], BF16)
        WB_t = sb("WB", [128, 24576], BF16)
        WK32_t = sb("WK32", [128, 12288], F32)
        WK16_t = sb("WK16", [128, 28672], BF16)
        ptab = Tl(sb("ptab", [128, NCOL], F32)[:])
        cmat = Tl(sb("cmat", [128, 512], F32)[:])
        cst = Tl(sb("cst", [128, 5 * 128], F32)[:])
        onesb = Tl(sb("onesb", [128, 128], BF16)[:])
        misc = Tl(sb("misc", [128, 16], F32)[:])
        epst = Tl(sb("epst", [128, 1], F32)[:])
        banks = [Tl(es.enter_context(nc.psum_tensor("pb%d" % i, [128, 512], F32))[:]) for i in range(8)]
        WA = Tl(WA_t[:])
        WB = Tl(WB_t[:])
        A32 = Arena(WK32_t[:], 4)
        A16 = Arena(WK16_t[:], 2)

        J_ap = cmat.ap[:, 0:128]
        PERM = cmat.ap[:, 128:256]
        IDN = cmat.ap[:, 256:384]
        ONES_D = cst.ap[:, 0:128]
        ONES_512 = cst.ap[:, 128:256]
        BLK64 = cst.ap[:, 256:384]
        ONES_128 = cst.ap[:, 384:512]
        ONES_1 = cst.ap[:, 512:640]

        def pcol(c):
            return ptab.ap[:, c:c + 1]

        def mm(out, lhsT, rhs, start, stop, r, w):
            S.op("pe", lambda e: e.matmul(out, lhsT=lhsT, rhs=rhs, start=start, stop=stop), r=r, w=w)

        def act(out, in_, func, r, w, bias=None, scale=None):
            kw = {}
            if bias is not None:
                kw["bias"] = bias
            if scale is not None:
                kw["scale"] = scale
            S.op("act", lambda e: e.activation(out=out, in_=in_, func=func, **kw), r=r, w=w)

        def tt(eng, out, in0, in1, op, r, w):
            S.op(eng, lambda e: e.tensor_tensor(out=out, in0=in0, in1=in1, op=op), r=r, w=w)

        def ts(eng, out, in0, s1, s2, op0, op1, r, w):
            if op1 is None:
                S.op(eng, lambda e: e.tensor_scalar(out=out, in0=in0, scalar1=s1, scalar2=None, op0=op0), r=r, w=w)
            else:
                S.op(eng, lambda e: e.tensor_scalar(out=out, in0=in0, scalar1=s1, scalar2=s2, op0=op0, op1=op1), r=r, w=w)

        def stt(eng, out, in0, sc, in1, op0, op1, r, w):
            S.op(eng, lambda e: e.scalar_tensor_tensor(out=out, in0=in0, scalar=sc, in1=in1, op0=op0, op1=op1), r=r, w=w)

        def cp(eng, out, in_, r, w):
            if eng == "act":
                S.op("act", lambda e: e.activation(out=out, in_=in_, func=AF.Copy), r=r, w=w)
            else:
                S.op(eng, lambda e: e.tensor_copy(out=out, in_=in_), r=r, w=w)

        def rsqrt_eps(ot, out, in_, it):
            act(out, in_, AF.Ln, r=(it, epst), w=(ot,), bias=epst.ap[0:out.shape[0], 0:1])
            act(out, out, AF.Exp, r=(ot,), w=(ot,), scale=-0.5)

        def recip(ot, out, in_, it, bias=None):
            if bias is None:
                act(out, in_, AF.Ln, r=(it,), w=(ot,))
            else:
                act(out, in_, AF.Ln, r=(it, misc), w=(ot,), bias=bias)
            act(out, out, AF.Exp, r=(ot,), w=(ot,), scale=-1.0)

        def memset(eng, t, ap, val):
            S.op(eng, lambda e: e.memset(ap, val), r=(), w=(t,))

        def phase():
            S.barrier()
            A32.reset()
            A16.reset()

        def load_x_bf(dst, src, g0, n, col0=0):
            S.dma("pool", dst.ap[:, :, col0:col0 + n], src.ap[:, g0:g0 + n].rearrange("(k p) n -> p k n", p=128), r=(src,), w=(dst,))

        def load_w(arena, view, src):
            S.dma("pool", view, src, r=(), w=(arena,))

        def layer_norm(zc, n, ones_ap, gcol, bcol, stat_banks, tmp, out_bf=None, func=AF.Identity, out_f32=True):
            nch = len(zc)
            b1, b2 = stat_banks
            for c in range(nch):
                mm(b1.ap[:, 0:n], ones_ap, zc[c].ap[:, 0:n], c == 0, c == nch - 1, r=(zc[c], cst), w=(b1,))
            for c in range(nch):
                sq = tmp[c % 2]
                act(sq.ap[:, 0:n], zc[c].ap[:, 0:n], AF.Square, r=(zc[c],), w=(sq,))
                mm(b2.ap[:, 0:n], ones_ap, sq.ap[:, 0:n], c == 0, c == nch - 1, r=(sq, cst), w=(b2,))
            m = tmp[2]
            v = tmp[3]
            cp("act", m.ap[:, 0:n], b1.ap[:, 0:n], r=(b1,), w=(m,))
            tt("dve", v.ap[:, 0:n], b1.ap[:, 0:n], m.ap[:, 0:n], ALU.mult, r=(b1, m), w=(v,))
            tt("dve", v.ap[:, 0:n], b2.ap[:, 0:n], v.ap[:, 0:n], ALU.subtract, r=(b2, v), w=(v,))
            rsqrt_eps(v, v.ap[:, 0:n], v.ap[:, 0:n], v)
            for c in range(nch):
                xc = tmp[4 + (c % 2)]
                tt("pool", xc.ap[:, 0:n], zc[c].ap[:, 0:n], m.ap[:, 0:n], ALU.subtract, r=(zc[c], m), w=(xc,))
                tt("dve", xc.ap[:, 0:n], xc.ap[:, 0:n], v.ap[:, 0:n], ALU.mult, r=(xc, v), w=(xc,))
                if out_f32:
                    act(zc[c].ap[:, 0:n], xc.ap[:, 0:n], func, r=(xc, ptab), w=(zc[c],), bias=pcol(bcol + c), scale=pcol(gcol + c))
                    if out_bf is not None:
                        cp("pool", out_bf.ap[:, c, 0:n], zc[c].ap[:, 0:n], r=(zc[c],), w=(out_bf,))
                else:
                    act(out_bf.ap[:, c, 0:n], xc.ap[:, 0:n], func, r=(xc, ptab), w=(out_bf,), bias=pcol(bcol + c), scale=pcol(gcol + c))

        def chunks(t, n):
            return [Tl(t.ap[:, c, :]) for c in range(n)]

        S.dma("sp", cmat.ap, cmat_d, w=(cmat,))
        memset("dve", cst, cst.ap[:, 0:128], 1.0 / 1024)
        memset("dve", cst, cst.ap[:, 128:256], 1.0 / 512)
        memset("dve", cst, cst.ap[:, 256:384], 0.0)
        memset("dve", cst, cst.ap[0:64, 256:320], 1.0 / 64)
        memset("dve", cst, cst.ap[64:128, 320:384], 1.0 / 64)
        memset("dve", cst, cst.ap[:, 384:512], 1.0 / 128)
        memset("dve", cst, cst.ap[:, 512:640], 1.0)
        memset("pool", onesb, onesb.ap, 1.0)
        memset("pool", epst, epst.ap, EPS)

        phase()
        rb = A32.take([33, 16])
        ohd = A32.take([33, TD_L])
        ohw = A32.take([33, TW_L])
        tsd = A32.take([16, TD_L])
        tsw = A32.take([16, TW_L])
        memset("dve", rb, rb.ap[32:33, :], 1.0)
        S.dma("sp", rb.ap[0:32, :], rel_bias, w=(rb,))
        S.dma("sp", ohd.ap, ohd_d, w=(ohd,))
        S.dma("sp", ohw.ap, ohw_d, w=(ohw,))
        for (oh, tsx, L, dst) in ((ohd, tsd, TD_L, TSD), (ohw, tsw, TW_L, TSW)):
            for i, c0 in enumerate(range(0, L, 512)):
                n = min(512, L - c0)
                bk = banks[i % 4]
                mm(bk.ap[0:16, 0:n], rb.ap[:, :], oh.ap[:, c0:c0 + n], True, True, r=(rb, oh), w=(bk,))
                cp("act", tsx.ap[:, c0:c0 + n], bk.ap[0:16, 0:n], r=(bk,), w=(tsx,))
            S.dma("sp", dst.ap, tsx.ap, r=(tsx,), w=(dst,))
        hk = [A32.take([128, TD_W]) for _ in range(2)]
        wt = [A32.take([128, TD_W]) for _ in range(2)]
        for m in range(16):
            if m < 8:
                src, L, Wd, dst = TSD, TD_L, TD_W, BTD.ap[m]
            else:
                src, L, Wd, dst = TSW, TW_L, TW_W, BTW.ap[m - 8]
            h = hk[m % 2]
            o = wt[m % 2]
            hank = bass.AP(src.ap.tensor, m * L, [[1, 128], [1, Wd]])
            S.dma("sp", h.ap[:, 0:Wd], hank, r=(src,), w=(h,))
            for i, c0 in enumerate(range(0, Wd, 512)):
                n = min(512, Wd - c0)
                bk = banks[4 + (i % 4)]
                mm(bk.ap[:, 0:n], J_ap, h.ap[:, c0:c0 + n], True, True, r=(cmat, h), w=(bk,))
                cp("act" if i % 2 == 0 else "dve", o.ap[:, c0:c0 + n], bk.ap[:, 0:n], r=(bk,), w=(o,))
            S.dma("sp", dst, o.ap[:, 0:Wd], r=(o,), w=(BTD if m < 8 else BTW,))

        phase()
        zt = A16.take([64, 2048])
        memset("pool", zt, zt.ap, 0.0)
        for c0 in range(0, TOK, 2048):
            n = min(2048, TOK - c0)
            for m in range(8):
                o = 1 - (m % 2)
                S.dma("sp", KD.ap[m, o * 64:(o + 1) * 64, c0:c0 + n], zt.ap[:, 0:n], r=(zt,), w=(KD,))
            for KX in (KW, KA):
                for kv in range(2):
                    for v in range(2):
                        o = 1 - v
                        S.dma("sp", KX.ap[kv, v, o * 64:(o + 1) * 64, c0:c0 + n], zt.ap[:, 0:n], r=(zt,), w=(KX,))

        for l in range(DEPTH):
            XIN = xT if l == 0 else X3
            XOUT = X3 if l < DEPTH - 1 else yT
            lam_init = 0.8 - 0.6 * math.exp(-0.3 * l)

            phase()
            S.dma("sp", ptab.ap, ptab_d[l], w=(ptab,))
            lamt = A32.take([1, 256])
            lt2 = A32.take([1, 128])
            lt3 = A32.take([1, 4])
            skt = A32.take([128, 8])
            S.dma("sp", lamt.ap, lam_d[l], w=(lamt,))
            S.dma("sp", skt.ap, sink_d[l], w=(skt,))
            lv = lamt.ap.rearrange("p (a b d) -> p a b d", a=2, b=2)
            tt("dve", lt2.ap.rearrange("p (a d) -> p a d", a=2), lv[:, :, 0, :], lv[:, :, 1, :], ALU.mult, r=(lamt,), w=(lt2,))
            S.op("dve", lambda e, o=lt3.ap[:, 0:2], i=lt2.ap.rearrange("p (a d) -> p a d", a=2): e.reduce_sum(out=o, in_=i, axis=AX.X), r=(lt2,), w=(lt3,))
            act(lt3.ap[:, 0:2], lt3.ap[:, 0:2], AF.Exp, r=(lt3,), w=(lt3,))
            tt("dve", lt3.ap[:, 2:3], lt3.ap[:, 0:1], lt3.ap[:, 1:2], ALU.subtract, r=(lt3,), w=(lt3,))
            ts("dve", lt3.ap[:, 3:4], lt3.ap[:, 2:3], lam_init, -1.0, ALU.add, ALU.mult, r=(lt3,), w=(lt3,))
            mm(banks[0].ap[:, 0:1], ONES_1[0:1, :], lt3.ap[:, 3:4], True, True, r=(cst, lt3), w=(banks[0],))
            cp("act", misc.ap[:, 0:1], banks[0].ap[:, 0:1], r=(banks[0],), w=(misc,))
            ts("dve", misc.ap[:, 1:2], pcol(C_SUBG), 1.0 - lam_init, None, ALU.mult, None, r=(ptab,), w=(misc,))
            act(misc.ap[:, 2:10], skt.ap, AF.Exp, r=(skt,), w=(misc,))

            phase()
            WAv = WA.ap.rearrange("p (k n) -> p k n", k=8)
            for k in range(8):
                load_w(WA, WAv[:, k, :], w_in[l, k * 128:(k + 1) * 128, 1024:4096])
            brow = A32.take([128, 768])
            S.dma("sp", brow.ap, brow_d[l], w=(brow,))
            xb2 = [A16.take([128, 8, 512]) for _ in range(2)]
            cs2 = [A32.take([128, 2, 512]) for _ in range(2)]
            stg = [A16.take([128, 4, 512]) for _ in range(3)]
            stgv = [A16.take([128, 4, 512]) for _ in range(2)]
            stgw = [A16.take([128, 4, 2, 2, 128]) if False else A16.take([128, 4, 512]) for _ in range(2)]
            tq = [A32.take([128, 512]) for _ in range(8)]
            for sw in stgw:
                memset("pool", sw, sw.ap, 1.0)
            fm_groups = [(O_BQ, 4, QD, 0), (O_BK, 4, KD, 0), (O_CQ, 4, QW, 0), (O_CK, 1, KW, 0),
                         (O_DQ, 4, QA, 1), (O_DK, 1, KA, 2)]
            sgi = 0
            ev = 0
            pbi = 0
            def pa_load(ti):
                s, l0, g0 = tiles[ti]
                load_x_bf(xb2[ti % 2], XIN, g0, 512)
                S.dma("sp", cs2[ti % 2].ap, rope_d[:, :, l0:l0 + 512].rearrange("a p n -> p a n"), w=(cs2[ti % 2],))

            pa_load(0)
            for ti, (s, l0, g0) in enumerate(tiles):
                if ti + 1 < NT:
                    pa_load(ti + 1)
                xb = xb2[ti % 2]
                cs = cs2[ti % 2]
                for (off, nchk, dst, kind) in fm_groups:
                    st = stg[sgi % 3]
                    sgi += 1
                    for c in range(nchk):
                        bk = banks[pbi % 4]
                        pbi += 1
                        wc = off - 1024 + c * 128
                        for k in range(8):
                            mm(bk.ap, WAv[:, k, wc:wc + 128], xb.ap[:, k, :], k == 0, k == 7, r=(WA, xb), w=(bk,))
                        bcolap = pcol(C_BIN + (off + c * 128) // 128)
                        if kind == 0:
                            if ev % 2 == 0:
                                act(st.ap[:, c, :], bk.ap, AF.Identity, r=(bk, ptab), w=(st,), bias=bcolap)
                            else:
                                ts("dve", st.ap[:, c, :], bk.ap, bcolap, None, ALU.add, None, r=(bk, ptab), w=(st,))
                            ev += 1
                        else:
                            q32, sq, rr, qn, aa, bb = tq[0], tq[1], tq[2], tq[3], tq[4], tq[5]
                            act(q32.ap, bk.ap, AF.Identity, r=(bk, ptab), w=(q32,), bias=bcolap)
                            act(sq.ap, bk.ap, AF.Square, r=(bk, ptab), w=(sq,), bias=bcolap)
                            b4 = banks[4]
                            b5 = banks[5]
                            mm(b4.ap, BLK64, sq.ap, True, True, r=(cst, sq), w=(b4,))
                            rsqrt_eps(rr, rr.ap, b4.ap, b4)
                            stt("dve", qn.ap, q32.ap, pcol(C_QNG if kind == 1 else C_KNG), rr.ap, ALU.mult, ALU.mult, r=(q32, rr, ptab), w=(qn,))
                            mm(b5.ap, PERM, qn.ap, True, True, r=(cmat, qn), w=(b5,))
                            tt("pool", aa.ap, qn.ap, cs.ap[:, 0, :], ALU.mult, r=(qn, cs), w=(aa,))
                            tt("dve", bb.ap, b5.ap, cs.ap[:, 1, :], ALU.mult, r=(b5, cs), w=(bb,))
                            tt("pool", st.ap[:, c, :], aa.ap, bb.ap, ALU.add, r=(aa, bb), w=(st,))
                    if dst is KW or dst is KA:
                        for kv in range(2):
                            for v in range(2):
                                S.dma("sp", dst.ap[kv, v, v * 64:(v + 1) * 64, g0:g0 + 512], st.ap[kv * 64:(kv + 1) * 64, 0, :], r=(st,), w=(dst,))
                    elif dst is KD:
                        for c in range(4):
                            for m2 in range(2):
                                S.dma("sp", KD.ap[2 * c + m2, m2 * 64:(m2 + 1) * 64, g0:g0 + 512], st.ap[m2 * 64:(m2 + 1) * 64, c, :], r=(st,), w=(KD,))
                    else:
                        S.dma("sp", dst.ap[:, g0:g0 + 512].rearrange("(c p) n -> p c n", p=128), st.ap[:, 0:nchk, :], r=(st,), w=(dst,))
                sv = stgv[ti % 2]
                sw = stgw[ti % 2]
                swv = sw.ap.rearrange("p j (a b d) -> p j a b d", a=2, b=2)
                for j in range(4):
                    b6 = banks[6]
                    b7 = banks[7]
                    for k in range(8):
                        mm(b6.ap, xb.ap[:, k, j * 128:(j + 1) * 128], WAv[:, k, O_BV - 1024:O_BV - 1024 + 512], k == 0, k == 7, r=(WA, xb), w=(b6,))
                    for k in range(8):
                        mm(b7.ap[:, 0:128], xb.ap[:, k, j * 128:(j + 1) * 128], WAv[:, k, O_CV - 1024:O_CV - 1024 + 128], k == 0, k == 7, r=(WA, xb), w=(b7,))
                    for k in range(8):
                        mm(b7.ap[:, 128:256], xb.ap[:, k, j * 128:(j + 1) * 128], WAv[:, k, O_DV - 1024:O_DV - 1024 + 128], k == 0, k == 7, r=(WA, xb), w=(b7,))
                    tt("dve", sv.ap[:, j, :], b6.ap, brow.ap[:, 0:512], ALU.add, r=(b6, brow), w=(sv,))
                    swj = sw.ap[:, j, :].rearrange("p (a d) -> p a d", a=4)
                    tt("dve", swj[:, :, 0:64], b7.ap[:, 0:256].rearrange("p (a d) -> p a d", a=4), brow.ap[:, 512:768].rearrange("p (a d) -> p a d", a=4), ALU.add, r=(b7, brow), w=(sw,))
                S.dma("sp", VD.ap[g0:g0 + 512, :].rearrange("(j p) f -> p j f", p=128), sv.ap, r=(sv,), w=(VD,))
                S.dma("sp", VW.ap[g0:g0 + 512].rearrange("(j p) a d -> p j (a d)", p=128), sw.ap[:, :, 0:256], r=(sw,), w=(VW,))
                S.dma("sp", VA.ap[g0:g0 + 512].rearrange("(j p) a d -> p j (a d)", p=128), sw.ap[:, :, 256:512], r=(sw,), w=(VA,))

            phase()
            WBv = WB.ap[:, 0:8192].rearrange("p (k n) -> p k n", k=8)
            Dg = WB.ap[:, 8192:8192 + 124 * 128].rearrange("p (j n) -> p j n", j=124)
            for k in range(8):
                load_w(WB, WBv[:, k, :], w_in[l, k * 128:(k + 1) * 128, 0:1024])
            for cj in range(124):
                ts("dve", Dg[:, cj, :], IDN, pcol(C_CW + cj), None, ALU.mult, None, r=(cmat, ptab), w=(WB,))
            xw2 = [A16.take([128, 8, 544]) for _ in range(2)]
            hb2 = [A16.take([128, 4, 544]) for _ in range(2)]
            sg2 = [A32.take([128, 544]) for _ in range(2)]
            acc = chunks(A32.take([128, 4, 512]), 4)
            tmpb = [A32.take([128, 512]) for _ in range(6)]
            sto = [A16.take([128, 4, 512]) for _ in range(2)]

            def pb_geom(ti):
                s, l0, g0 = tiles[ti]
                lo = max(l0 - 15, 0)
                hi = min(l0 + 527, seqs[s])
                return lo, hi - lo, lo - (l0 - 15)

            def pb_load(ti):
                s, l0, g0 = tiles[ti]
                lo, n, c0 = pb_geom(ti)
                load_x_bf(xw2[ti % 2], XIN, g0 - l0 + lo, n, col0=c0)

            def pb_stage1(ti):
                lo, n, c0 = pb_geom(ti)
                xw = xw2[ti % 2]
                hb = hb2[ti % 2]
                if c0 > 0:
                    memset("pool", hb, hb.ap[:, :, 0:c0], 0.0)
                if c0 + n < 542:
                    memset("pool", hb, hb.ap[:, :, c0 + n:542], 0.0)
                n1 = min(n, 512)
                n2 = n - n1
                for c in range(4):
                    bA, bG = banks[(c % 2) * 2], banks[(c % 2) * 2 + 1]
                    b4 = banks[4]
                    sg = sg2[c % 2]
                    for (bk, wc) in ((bA, c * 128), (bG, 512 + c * 128)):
                        for k in range(8):
                            mm(bk.ap[:, 0:n1], WBv[:, k, wc:wc + 128], xw.ap[:, k, c0:c0 + n1], k == 0, k == 7, r=(WB, xw), w=(bk,))
                    if n2 > 0:
                        for (co, wc) in ((0, c * 128), (64, 512 + c * 128)):
                            for k in range(8):
                                mm(b4.ap[:, co:co + n2], WBv[:, k, wc:wc + 128], xw.ap[:, k, c0 + n1:c0 + n], k == 0, k == 7, r=(WB, xw), w=(b4,))
                    act(sg.ap[:, 0:n1], bG.ap[:, 0:n1], AF.Sigmoid, r=(bG, ptab), w=(sg,), bias=pcol(C_BIN + 4 + c))
                    stt("dve", hb.ap[:, c, c0:c0 + n1], bA.ap[:, 0:n1], pcol(C_BIN + c), sg.ap[:, 0:n1], ALU.add, ALU.mult, r=(bA, sg, ptab), w=(hb,))
                    if n2 > 0:
                        act(sg.ap[:, 512:512 + n2], b4.ap[:, 64:64 + n2], AF.Sigmoid, r=(b4, ptab), w=(sg,), bias=pcol(C_BIN + 4 + c))
                        stt("dve", hb.ap[:, c, c0 + n1:c0 + n], b4.ap[:, 0:n2], pcol(C_BIN + c), sg.ap[:, 512:512 + n2], ALU.add, ALU.mult, r=(b4, sg, ptab), w=(hb,))

            def pb_stage2(ti):
                s, l0, g0 = tiles[ti]
                hb = hb2[ti % 2]
                for c in range(4):
                    bk = banks[5 + (c % 2)]
                    for j in range(31):
                        mm(bk.ap, Dg[:, c * 31 + j, :], hb.ap[:, c, j:j + 512], j == 0, j == 30, r=(WB, hb), w=(bk,))
                    if c % 2 == 0:
                        act(acc[c].ap, bk.ap, AF.Identity, r=(bk, ptab), w=(acc[c],), bias=pcol(C_CB + c))
                    else:
                        ts("dve", acc[c].ap, bk.ap, pcol(C_CB + c), None, ALU.add, None, r=(bk, ptab), w=(acc[c],))
                so = sto[ti % 2]
                layer_norm(acc, 512, ONES_512, C_ALG, C_ALB, (banks[7], banks[5]), tmpb, out_bf=so, func=AF.Silu, out_f32=False)
                S.dma("sp", OBR[0].ap[:, g0:g0 + 512].rearrange("(c p) n -> p c n", p=128), so.ap, r=(so,), w=(OBR[0],))

            pb_load(0)
            if NT > 1:
                pb_load(1)
            pb_stage1(0)
            for ti in range(NT):
                if ti + 1 < NT:
                    pb_stage1(ti + 1)
                if ti + 2 < NT:
                    pb_load(ti + 2)
                pb_stage2(ti)

            phase()
            kt2 = [A16.take([128, 2, SMAX]) for _ in range(2)]
            vv2 = [A16.take([128, SMAX // 128, 128]) for _ in range(2)]
            qt2 = [A16.take([128, 512]) for _ in range(2)]
            pt = [A16.take([128, 512]) for _ in range(4)]
            ost = [A16.take([128, 512]) for _ in range(2)]
            bt2 = [A32.take([128, 2, TD_W]) for _ in range(2)]
            tn = [A32.take([128, 512]) for _ in range(3)]
            fz = [A32.take([128, 512]) for _ in range(8)]
            hi_ = 0
            qi_ = 0
            pi_ = 0
            ni_ = 0
            for s, Sq in enumerate(seqs):
                G0 = goff[s]
                NB = Sq // 128
                for h in range(4):
                    kt = kt2[hi_ % 2]
                    vv = vv2[hi_ % 2]
                    bt = bt2[hi_ % 2]
                    hi_ += 1
                    S.dma("sp", kt.ap[:, :, 0:Sq], KD.ap[2 * h:2 * h + 2, :, G0:G0 + Sq].rearrange("m p s -> p m s"), r=(KD,), w=(kt,))
                    S.dma("sp", vv.ap[:, 0:NB, :], VD.ap[G0:G0 + Sq, h * 128:(h + 1) * 128].rearrange("(b p) e -> p b e", p=128), r=(VD,), w=(vv,))
                    S.dma("sp", bt.ap, BTD.ap[2 * h:2 * h + 2].rearrange("m p w -> p m w"), r=(BTD,), w=(bt,))
                    for qt_i in range(Sq // 512):
                        q0 = qt_i * 512
                        qt = qt2[qi_ % 2]
                        qi_ += 1
                        S.dma("sp", qt.ap, QD.ap[h * 128:(h + 1) * 128, G0 + q0:G0 + q0 + 512], r=(QD,), w=(qt,))
                        items = [(kb, m2) for kb in range(NB) for m2 in range(2)]
                        acc_o = (banks[3], banks[4])
                        acc_s = (banks[5], banks[6])

                        def score(it, idx):
                            kb, m2 = it
                            bk = banks[idx % 3]
                            mm(bk.ap, kt.ap[:, m2, kb * 128:(kb + 1) * 128], qt.ap, True, True, r=(kt, qt), w=(bk,))

                        LOOK = 2
                        for i in range(min(LOOK, len(items))):
                            score(items[i], i)
                        for idx, (kb, m2) in enumerate(items):
                            bk = banks[idx % 3]
                            p = pt[pi_ % 4]
                            pi_ += 1
                            d = kb * 128 - q0
                            if d >= 602:
                                act(p.ap, bk.ap, AF.Exp, r=(bk, bt), w=(p,), bias=bt.ap[:, m2, 0:1], scale=0.125)
                            elif d <= -218:
                                act(p.ap, bk.ap, AF.Exp, r=(bk, bt), w=(p,), bias=bt.ap[:, m2, TD_W - 1:TD_W], scale=0.125)
                            else:
                                t_ = tn[ni_ % 3]
                                ni_ += 1
                                j0 = q0 - kb * 128 + 512
                                stt("dve", t_.ap, bk.ap, 0.125, bt.ap[:, m2, j0:j0 + 512], ALU.mult, ALU.add, r=(bk, bt), w=(t_,))
                                act(p.ap, t_.ap, AF.Exp, r=(t_,), w=(p,))
                            if idx + LOOK < len(items):
                                score(items[idx + LOOK], idx + LOOK)
                            mm(acc_o[m2].ap, vv.ap[:, kb, :], p.ap, kb == 0, kb == NB - 1, r=(vv, p), w=(acc_o[m2],))
                            mm(acc_s[m2].ap, onesb.ap, p.ap, kb == 0, kb == NB - 1, r=(onesb, p), w=(acc_s[m2],))
                        r0, r1, t0, t1, oo, sq, rr = fz[0], fz[1], fz[2], fz[3], fz[4], fz[5], fz[6]
                        recip(r0, r0.ap, acc_s[0].ap, acc_s[0])
                        recip(r1, r1.ap, acc_s[1].ap, acc_s[1])
                        tt("dve", t0.ap, acc_o[0].ap, r0.ap, ALU.mult, r=(acc_o[0], r0), w=(t0,))
                        tt("dve", t1.ap, acc_o[1].ap, r1.ap, ALU.mult, r=(acc_o[1], r1), w=(t1,))
                        stt("dve", oo.ap, t1.ap, misc.ap[:, 0:1], t0.ap, ALU.mult, ALU.add, r=(t0, t1, misc), w=(oo,))
                        act(sq.ap, oo.ap, AF.Square, r=(oo,), w=(sq,))
                        b7 = banks[7]
                        mm(b7.ap, ONES_128, sq.ap, True, True, r=(cst, sq), w=(b7,))
                        rsqrt_eps(rr, rr.ap, b7.ap, b7)
                        os_ = ost[qi_ % 2]
                        stt("dve", os_.ap, oo.ap, misc.ap[:, 1:2], rr.ap, ALU.mult, ALU.mult, r=(oo, rr, misc), w=(os_,))
                        S.dma("sp", OBR[1].ap[h * 128:(h + 1) * 128, G0 + q0:G0 + q0 + 512], os_.ap, r=(os_,), w=(OBR[1],))

            for which in ("win", "ax"):
                phase()
                QX, KX, VX, OX = (QW, KW, VW, OBR[2]) if which == "win" else (QA, KA, VA, OBR[3])
                kt2 = [A16.take([128, 2, SMAX]) for _ in range(2)]
                vv2 = [A16.take([128, SMAX // 128, 128]) for _ in range(2)]
                qt2 = [A16.take([128, 512]) for _ in range(3)]
                pt = [A16.take([128, 512]) for _ in range(3)]
                ost = [A16.take([128, 512]) for _ in range(2)]
                btw = [A32.take([128, 4, TW_W]) for _ in range(2)]
                tn = [A32.take([128, 384]) for _ in range(3)]
                rc = [A32.take([128, 512]) for _ in range(2)]
                rs = [A32.take([128, 512]) for _ in range(2)]
                hi_ = qi_ = pi_ = ni_ = ai_ = sci_ = 0
                pending = [None]

                def flush():
                    if pending[0] is not None:
                        f = pending[0]
                        pending[0] = None
                        f()

                for s, Sq in enumerate(seqs):
                    G0 = goff[s]
                    NB = Sq // 128
                    for kv in range(2):
                        kt = kt2[hi_ % 2]
                        vv = vv2[hi_ % 2]
                        bt = btw[hi_ % 2]
                        hi_ += 1
                        S.dma("sp", kt.ap[:, :, 0:Sq], KX.ap[kv, :, :, G0:G0 + Sq].rearrange("v p s -> p v s"), r=(KX,), w=(kt,))
                        S.dma("sp", vv.ap[:, 0:NB, :], VX.ap[G0:G0 + Sq, kv, :].rearrange("(b p) e -> p b e", p=128), r=(VX,), w=(vv,))
                        if which == "win":
                            S.dma("sp", bt.ap, BTW.ap[kv * 4:kv * 4 + 4].rearrange("m p w -> p m w"), r=(BTW,), w=(bt,))
                        for g in range(4):
                            h = kv * 4 + g
                            vsel = h % 2
                            for qt_i in range(Sq // 512):
                                q0 = qt_i * 512
                                qt = qt2[qi_ % 3]
                                qi_ += 1
                                S.dma("sp", qt.ap, QX.ap[(h // 2) * 128:(h // 2) * 128 + 128, G0 + q0:G0 + q0 + 512], r=(QX,), w=(qt,))
                                accb = banks[4 + (ai_ % 2)]
                                b7 = banks[6 + (ai_ % 2)]
                                rcx = rc[ai_ % 2]
                                rsx = rs[ai_ % 2]
                                os_ = ost[ai_ % 2]
                                ai_ += 1
                                if which == "ax":
                                    def score(kb, kt=kt, qt=qt, vsel=vsel):
                                        nonlocal sci_
                                        bk = banks[sci_ % 4]
                                        sci_ += 1
                                        mm(bk.ap, kt.ap[:, vsel, kb * 128:(kb + 1) * 128], qt.ap, True, True, r=(kt, qt), w=(bk,))
                                        return bk
                                    LOOK = 3
                                    sb_ = [score(i) for i in range(min(LOOK, NB))]
                                    flush()
                                    for kb in range(NB):
                                        bk = sb_[kb]
                                        p = pt[pi_ % 3]
                                        pi_ += 1
                                        act(p.ap, bk.ap, AF.Exp, r=(bk,), w=(p,), scale=0.125)
                                        if kb + LOOK < NB:
                                            sb_.append(score(kb + LOOK))
                                        mm(accb.ap, vv.ap[:, kb, :], p.ap, kb == 0, kb == NB - 1, r=(vv, p), w=(accb,))
                                else:
                                    def wscore(qbl, kt=kt, qt=qt, vsel=vsel, qt_i=qt_i, NB=NB):
                                        nonlocal sci_
                                        qb = qt_i * 4 + qbl
                                        kbs = [kb for kb in (qb - 1, qb, qb + 1) if 0 <= kb < NB]
                                        bk = banks[sci_ % 4]
                                        sci_ += 1
                                        slots = []
                                        for kb in kbs:
                                            sl = 2 - (kb - qb + 1)
                                            slots.append(sl)
                                            mm(bk.ap[:, sl * 128:(sl + 1) * 128], kt.ap[:, vsel, kb * 128:(kb + 1) * 128], qt.ap[:, qbl * 128:(qbl + 1) * 128], True, True, r=(kt, qt), w=(bk,))
                                        return bk, kbs, slots
                                    nxt = wscore(0)
                                    flush()
                                    for qbl in range(4):
                                        bk, kbs, slots = nxt
                                        c_lo, c_hi = min(slots) * 128, (max(slots) + 1) * 128
                                        t_ = tn[ni_ % 3]
                                        ni_ += 1
                                        p = pt[pi_ % 3]
                                        pi_ += 1
                                        stt("dve", t_.ap[:, c_lo:c_hi], bk.ap[:, c_lo:c_hi], 0.125, bt.ap[:, g, c_lo:c_hi], ALU.mult, ALU.add, r=(bk, bt), w=(t_,))
                                        act(p.ap[:, c_lo:c_hi], t_.ap[:, c_lo:c_hi], AF.Exp, r=(t_,), w=(p,))
                                        if qbl + 1 < 4:
                                            nxt = wscore(qbl + 1)
                                        for i, kb in enumerate(kbs):
                                            sl = slots[i]
                                            mm(accb.ap[:, qbl * 128:(qbl + 1) * 128], vv.ap[:, kb, :], p.ap[:, sl * 128:(sl + 1) * 128], i == 0, i == len(kbs) - 1, r=(vv, p), w=(accb,))

                                def fin(accb=accb, b7=b7, rcx=rcx, rsx=rsx, os_=os_, h=h, G0=G0, q0=q0):
                                    if which == "win":
                                        recip(rcx, rcx.ap[64:128, :], accb.ap[64:128, :], accb, bias=misc.ap[64:128, 2 + h:3 + h])
                                    else:
                                        recip(rcx, rcx.ap[64:128, :], accb.ap[64:128, :], accb)
                                    mm(b7.ap[0:64, :], IDN[64:128, 64:128], rcx.ap[64:128, :], True, True, r=(cmat, rcx), w=(b7,))
                                    cp("act", rsx.ap[0:64, :], b7.ap[0:64, :], r=(b7,), w=(rsx,))
                                    tt("dve", os_.ap[0:64, :], accb.ap[0:64, :], rsx.ap[0:64, :], ALU.mult, r=(accb, rsx), w=(os_,))
                                    S.dma("sp", OX.ap[h * 64:(h + 1) * 64, G0 + q0:G0 + q0 + 512], os_.ap[0:64, :], r=(os_,), w=(OX,))
                                pending[0] = fin
                flush()

            for hf in range(2):
                phase()
                AR = WA if hf == 0 else WB
                Wg = AR.ap[:, 0:16384].rearrange("p (k n c) -> p k n c", k=8, n=4)
                Wb = AR.ap[:, 16384:24576].rearrange("p (n k c) -> p n k c", n=4, k=4)
                for n in range(4):
                    c0 = O_GL + n * 1024 + hf * 512
                    load_w(AR, Wg[:, :, n, :], w_in[l, :, c0:c0 + 512].rearrange("(k p) c -> p k c", p=128))
                    load_w(AR, Wb[:, n, :, :], w_branch[l, n, :, hf * 512:(hf + 1) * 512].rearrange("(k p) c -> p k c", p=128))
                xb2 = [A16.take([128, 8, 512]) for _ in range(2)]
                ob = [A16.take([128, 4, 512]) for _ in range(4)]
                stg = [A16.take([128, 4, 512]) for _ in range(2)]
                sgt = [A32.take([128, 512]) for _ in range(3)]
                tmt = [A32.take([128, 512]) for _ in range(3)]
                mgt = [A32.take([128, 512]) for _ in range(2)]
                gi = 0
                load_x_bf(xb2[0], XIN, tiles[0][2], 512)
                for ti, (s, l0, g0) in enumerate(tiles):
                    xb = xb2[ti % 2]
                    if ti + 1 < NT:
                        load_x_bf(xb2[(ti + 1) % 2], XIN, tiles[ti + 1][2], 512)
                    for n in range(4):
                        S.dma("sp", ob[n].ap, OBR[n].ap[:, g0:g0 + 512].rearrange("(k p) n -> p k n", p=128), r=(OBR[n],), w=(ob[n],))
                    st = stg[ti % 2]
                    for cc in range(4):
                        mg = mgt[cc % 2]
                        for n in range(4):
                            bG = banks[(gi % 2) * 2]
                            bP = banks[(gi % 2) * 2 + 1]
                            sg = sgt[gi % 3]
                            tm = tmt[gi % 3]
                            gi += 1
                            for k in range(8):
                                mm(bG.ap, Wg[:, k, n, cc * 128:(cc + 1) * 128], xb.ap[:, k, :], k == 0, k == 7, r=(AR, xb), w=(bG,))
                            for k in range(4):
                                mm(bP.ap, Wb[:, n, k, cc * 128:(cc + 1) * 128], ob[n].ap[:, k, :], k == 0, k == 3, r=(AR, ob[n]), w=(bP,))
                            gcol = pcol(C_BIN + (O_GL + n * 1024 + hf * 512 + cc * 128) // 128)
                            act(sg.ap, bG.ap, AF.Sigmoid, r=(bG, ptab), w=(sg,), bias=gcol)
                            if n == 0:
                                tt("dve", mg.ap, bP.ap, sg.ap, ALU.mult, r=(bP, sg), w=(mg,))
                            else:
                                tt("dve", tm.ap, bP.ap, sg.ap, ALU.mult, r=(bP, sg), w=(tm,))
                                if n < 3:
                                    tt("pool", mg.ap, mg.ap, tm.ap, ALU.add, r=(mg, tm), w=(mg,))
                                else:
                                    tt("pool", st.ap[:, cc, :], mg.ap, tm.ap, ALU.add, r=(mg, tm), w=(st,))
                    S.dma("sp", MG.ap[hf * 512:(hf + 1) * 512, g0:g0 + 512].rearrange("(c p) n -> p c n", p=128), st.ap, r=(st,), w=(MG,))

            phase()
            Wkv = WA.ap[:, 0:16384].rearrange("p (k n) -> p k n", k=8)
            for k in range(8):
                load_w(WA, Wkv[:, k, :], w_xkv[l, k * 128:(k + 1) * 128, :])
            mt2 = [A16.take([128, 8, MEM]) for _ in range(2)]
            mks = [A16.take([128, 8, MEM]) for _ in range(2)]
            mvs = [A16.take([128, 2, D]) for _ in range(2)]
            pbi = 0
            for s in range(NSEQ):
                mt = mt2[s % 2]
                S.dma("pool", mt.ap, memT[:, s * MEM:(s + 1) * MEM].rearrange("(k p) n -> p k n", p=128), r=(), w=(mt,))
                mk = mks[s % 2]
                mv = mvs[s % 2]
                for c in range(8):
                    bk = banks[pbi % 4]
                    pbi += 1
                    for k in range(8):
                        mm(bk.ap[:, 0:MEM], Wkv[:, k, c * 128:(c + 1) * 128], mt.ap[:, k, :], k == 0, k == 7, r=(WA, mt), w=(bk,))
                    cp("act" if c % 2 == 0 else "dve", mk.ap[:, c, :], bk.ap[:, 0:MEM], r=(bk,), w=(mk,))
                for j in range(2):
                    for hh in range(2):
                        bk = banks[pbi % 4]
                        pbi += 1
                        for k in range(8):
                            mm(bk.ap, mt.ap[:, k, j * 128:(j + 1) * 128], Wkv[:, k, 1024 + hh * 512:1024 + (hh + 1) * 512], k == 0, k == 7, r=(WA, mt), w=(bk,))
                        cp("act" if hh == 0 else "dve", mv.ap[:, j, hh * 512:(hh + 1) * 512], bk.ap, r=(bk,), w=(mv,))
                S.dma("sp", MK.ap[s].rearrange("(c p) n -> p c n", p=128), mk.ap, r=(mk,), w=(MK,))
                S.dma("sp", MV.ap[s].rearrange("(j p) f -> p j f", p=128), mv.ap, r=(mv,), w=(MV,))

            phase()
            W3 = WB.ap.rearrange("p (w k n) -> p w k n", w=3, k=8)
            for wi, wsrc in enumerate((w_mix_out, w_xq, w_xo)):
                for k in range(8):
                    load_w(WB, W3[:, wi, k, :], wsrc[l, k * 128:(k + 1) * 128, :])
            xs2 = [A32.take([128, 8, 512]) for _ in range(2)]
            xsc2 = [chunks(t, 8) for t in xs2]
            tmpm = [A32.take([128, 512]) for _ in range(6)]
            rcp = A32.take([128, 512])
            mgb2 = [A16.take([128, 8, 512]) for _ in range(2)]
            x1b = A16.take([128, 8, 512])
            qb = A16.take([128, 8, 512])
            ob2 = A16.take([128, 8, 512])
            pt = [A16.take([128, 512]) for _ in range(4)]
            mkt = A16.take([128, 8, MEM])
            mvt = A16.take([128, 2, D])
            cur_s = -1
            pbi = 0
            pi_ = 0

            def pm2_load(ti):
                g0 = tiles[ti][2]
                S.dma("sp", xs2[ti % 2].ap, XIN.ap[:, g0:g0 + 512].rearrange("(k p) n -> p k n", p=128), r=(XIN,), w=tuple(xsc2[ti % 2]))
                S.dma("sp", mgb2[ti % 2].ap, MG.ap[:, g0:g0 + 512].rearrange("(k p) n -> p k n", p=128), r=(MG,), w=(mgb2[ti % 2],))

            pm2_load(0)
            for ti, (s, l0, g0) in enumerate(tiles):
                if s != cur_s:
                    cur_s = s
                    S.dma("sp", mkt.ap, MK.ap[s].rearrange("(c p) n -> p c n", p=128), r=(MK,), w=(mkt,))
                    S.dma("sp", mvt.ap, MV.ap[s].rearrange("(j p) f -> p j f", p=128), r=(MV,), w=(mvt,))
                if ti + 1 < NT:
                    pm2_load(ti + 1)
                xs = xs2[ti % 2]
                xsc = xsc2[ti % 2]
                mgb = mgb2[ti % 2]

                def dense_res(wi, src):
                    nonlocal pbi
                    for c in range(8):
                        bk = banks[pbi % 3]
                        pbi += 1
                        for k in range(8):
                            mm(bk.ap, W3[:, wi, k, c * 128:(c + 1) * 128], src.ap[:, k, :], k == 0, k == 7, r=(WB, src), w=(bk,))
                        stt("dve", xsc[c].ap, xsc[c].ap, ALPHA, bk.ap, ALU.mult, ALU.add, r=(xsc[c], bk), w=(xsc[c],))

                dense_res(0, mgb)
                layer_norm(xsc, 512, ONES_D, C_L1G, C_L1B, (banks[3], banks[4]), tmpm, out_bf=x1b)
                for c in range(8):
                    bk = banks[pbi % 3]
                    pbi += 1
                    for k in range(8):
                        mm(bk.ap, W3[:, 1, k, c * 128:(c + 1) * 128], x1b.ap[:, k, :], k == 0, k == 7, r=(WB, x1b), w=(bk,))
                    cp("act" if c % 2 == 0 else "dve", qb.ap[:, c, :], bk.ap, r=(bk,), w=(qb,))
                for h in range(4):
                    acc_o = (banks[5], banks[6])
                    acc_s = banks[7]
                    for kb in range(2):
                        bk = banks[pbi % 3]
                        pbi += 1
                        for dc in range(2):
                            mm(bk.ap, mkt.ap[:, h * 2 + dc, kb * 128:(kb + 1) * 128], qb.ap[:, h * 2 + dc, :], dc == 0, dc == 1, r=(mkt, qb), w=(bk,))
                        p = pt[pi_ % 4]
                        pi_ += 1
                        act(p.ap, bk.ap, AF.Exp, r=(bk,), w=(p,), scale=1.0 / 16)
                        for ec in range(2):
                            mm(acc_o[ec].ap, mvt.ap[:, kb, (h * 2 + ec) * 128:(h * 2 + ec + 1) * 128], p.ap, kb == 0, kb == 1, r=(mvt, p), w=(acc_o[ec],))
                        mm(acc_s.ap, onesb.ap, p.ap, kb == 0, kb == 1, r=(onesb, p), w=(acc_s,))
                    recip(rcp, rcp.ap, acc_s.ap, acc_s)
                    for ec in range(2):
                        tt("dve", ob2.ap[:, h * 2 + ec, :], acc_o[ec].ap, rcp.ap, ALU.mult, r=(acc_o[ec], rcp), w=(ob2,))
                dense_res(2, ob2)
                layer_norm(xsc, 512, ONES_D, C_L2G, C_L2B, (banks[3], banks[4]), tmpm)
                S.dma("sp", X2.ap[:, g0:g0 + 512].rearrange("(k p) n -> p k n", p=128), xs.ap, r=tuple(xsc), w=(X2,))

            for hf in range(2):
                phase()
                AR = WA if hf == 0 else WB
                Wu = AR.ap[:, 0:8 * 2816].rearrange("p (k n) -> p k n", k=8)
                gc0 = hf * 11 * 128
                ng = min(11 * 128, FFN - gc0)
                for k in range(8):
                    load_w(AR, Wu[:, k, 0:ng], w_up[l, k * 128:(k + 1) * 128, gc0:gc0 + ng])
                    load_w(AR, Wu[:, k, 1408:1408 + ng], w_up[l, k * 128:(k + 1) * 128, FFN + gc0:FFN + gc0 + ng])
                xw2 = [A16.take([128, 8, 516]) for _ in range(2)]
                stf = [A16.take([128, 11, 512]) for _ in range(2)]
                hg = [A32.take([128, 516]) for _ in range(2)]
                hu = [A32.take([128, 516]) for _ in range(2)]
                cg = [A32.take([128, 512]) for _ in range(2)]
                cu = [A32.take([128, 512]) for _ in range(2)]
                gi = 0

                def pf_geom(ti):
                    s, l0, g0 = tiles[ti]
                    lo = max(l0 - 1, 0)
                    hi = min(l0 + 513, seqs[s])
                    return lo, hi - lo, lo - (l0 - 1)

                def pf_load(ti):
                    s, l0, g0 = tiles[ti]
                    lo, n, c0 = pf_geom(ti)
                    load_x_bf(xw2[ti % 2], X2, g0 - l0 + lo, n, col0=c0)

                pf_load(0)
                for ti, (s, l0, g0) in enumerate(tiles):
                    if ti + 1 < NT:
                        pf_load(ti + 1)
                    lo, n, c0 = pf_geom(ti)
                    xw = xw2[ti % 2]
                    n1 = min(n, 512)
                    n2 = n - n1
                    st = stf[ti % 2]
                    for ci in range(11):
                        M = min(128, ng - ci * 128)
                        fi = hf * 11 + ci
                        bg_, bu_ = banks[(gi % 2) * 2], banks[(gi % 2) * 2 + 1]
                        bg2, bu2 = banks[4 + (gi % 2) * 2], banks[5 + (gi % 2) * 2]
                        hgt, hut, cgt, cut = hg[gi % 2], hu[gi % 2], cg[gi % 2], cu[gi % 2]
                        gi += 1
                        for (bk, bk2, wc) in ((bg_, bg2, ci * 128), (bu_, bu2, 1408 + ci * 128)):
                            for k in range(8):
                                mm(bk.ap[0:M, 0:n1], Wu[:, k, wc:wc + M], xw.ap[:, k, c0:c0 + n1], k == 0, k == 7, r=(AR, xw), w=(bk,))
                            if n2 > 0:
                                for k in range(8):
                                    mm(bk2.ap[0:M, 0:n2], Wu[:, k, wc:wc + M], xw.ap[:, k, c0 + n1:c0 + n], k == 0, k == 7, r=(AR, xw), w=(bk2,))
                        for (ht, bk, bk2, eng) in ((hgt, bg_, bg2, "act"), (hut, bu_, bu2, "act")):
                            if c0 > 0:
                                memset("pool", ht, ht.ap[0:M, 0:c0], 0.0)
                            if c0 + n < 514:
                                memset("pool", ht, ht.ap[0:M, c0 + n:514], 0.0)
                            cp(eng, ht.ap[0:M, c0:c0 + n1], bk.ap[0:M, 0:n1], r=(bk,), w=(ht,))
                            if n2 > 0:
                                cp(eng, ht.ap[0:M, c0 + n1:c0 + n], bk2.ap[0:M, 0:n2], r=(bk2,), w=(ht,))
                        for (ht, ct, ii) in ((hgt, cgt, fi), (hut, cut, 22 + fi)):
                            act(ct.ap[0:M, :], ht.ap[0:M, 1:513], AF.Identity, r=(ht, ptab), w=(ct,), bias=pcol(C_FB + ii)[0:M], scale=pcol(C_FW + ii * 3 + 1)[0:M])
                            for j in (0, 2):
                                stt("dve", ct.ap[0:M, :], ht.ap[0:M, j:j + 512], pcol(C_FW + ii * 3 + j)[0:M], ct.ap[0:M, :], ALU.mult, ALU.add, r=(ht, ct, ptab), w=(ct,))
                        act(cgt.ap[0:M, :], cgt.ap[0:M, :], AF.Gelu, r=(cgt,), w=(cgt,))
                        tt("dve", st.ap[0:M, ci, :], cgt.ap[0:M, :], cut.ap[0:M, :], ALU.mult, r=(cgt, cut), w=(st,))
                        if M < 128:
                            memset("pool", st, st.ap[M:128, ci, :], 0.0)
                    S.dma("sp", ACS.ap[hf * 1408:(hf + 1) * 1408, g0:g0 + 512].rearrange("(c p) n -> p c n", p=128), st.ap, r=(st,), w=(ACS,))

            phase()
            Wd = WA.ap[:, 0:22 * 1024].rearrange("p (k n) -> p k n", k=22)
            for k in range(22):
                M = min(128, FFN - k * 128)
                load_w(WA, Wd[0:M, k, :], w_down[l, k * 128:k * 128 + M, :])
            xs2 = [A32.take([128, 8, 512]) for _ in range(2)]
            xsc2 = [chunks(t, 8) for t in xs2]
            tmpm = [A32.take([128, 512]) for _ in range(6)]
            at2 = [A16.take([128, 22, 512]) for _ in range(2)]
            pbi = 0

            def pf2_load(ti):
                g0 = tiles[ti][2]
                S.dma("sp", xs2[ti % 2].ap, X2.ap[:, g0:g0 + 512].rearrange("(k p) n -> p k n", p=128), r=(X2,), w=tuple(xsc2[ti % 2]))
                S.dma("sp", at2[ti % 2].ap, ACS.ap[:, g0:g0 + 512].rearrange("(k p) n -> p k n", p=128), r=(ACS,), w=(at2[ti % 2],))

            pf2_load(0)
            for ti, (s, l0, g0) in enumerate(tiles):
                if ti + 1 < NT:
                    pf2_load(ti + 1)
                xs = xs2[ti % 2]
                xsc = xsc2[ti % 2]
                at = at2[ti % 2]
                for c in range(8):
                    bk = banks[pbi % 3]
                    pbi += 1
                    for k in range(22):
                        M = min(128, FFN - k * 128)
                        mm(bk.ap, Wd[0:M, k, c * 128:(c + 1) * 128], at.ap[0:M, k, :], k == 0, k == 21, r=(WA, at), w=(bk,))
                    stt("dve", xsc[c].ap, xsc[c].ap, ALPHA, bk.ap, ALU.mult, ALU.add, r=(xsc[c], bk), w=(xsc[c],))
                layer_norm(xsc, 512, ONES_D, C_L3G, C_L3B, (banks[3], banks[4]), tmpm)
                S.dma("sp", XOUT.ap[:, g0:g0 + 512].rearrange("(k p) n -> p k n", p=128), xs.ap, r=tuple(xsc), w=(XOUT,))

        S.barrier()
        S.emit()
    return nc


def host_consts(smax):
    cm = np.zeros((128, 512), np.float32)
    for p in range(128):
        cm[p, 127 - p] = 1.0
        cm[p, 256 + p] = 1.0
    for d in range(128):
        dd = d % 32
        partner = d + 16 if dd < 16 else d - 16
        cm[partner, 128 + d] = 1.0
    t = np.arange(smax)
    row = (t // 64).astype(np.float32)
    col = (t % 64).astype(np.float32)
    inv = (np.float32(10000.0) ** (-np.arange(16, dtype=np.float32) / np.float32(16))).astype(np.float32)
    rope = np.zeros((2, 128, smax), np.float32)
    for p in range(128):
        d = p % 64
        pos = row if d < 32 else col
        dd = d % 32
        ang = (pos * inv[dd % 16]).astype(np.float32)
        rope[0, p] = np.cos(ang)
        rope[1, p] = -np.sin(ang) if dd < 16 else np.sin(ang)
    i = np.arange(TD_L)
    bd = t5_bucket_np((127 + 512 - i).astype(np.int32))
    ohd = np.zeros((33, TD_L), np.float32)
    ohd[bd, i] = 1.0
    i = np.arange(TW_L)
    relw = (255 - i).astype(np.int32)
    bw = t5_bucket_np(relw)
    ohw = np.zeros((33, TW_L), np.float32)
    ohw[bw, i] = 1.0
    ohw[32, :] = np.where(np.abs(relw) > 128, NEGM, 0.0)
    return cm, rope, ohd, ohw


def host_ptab(inp):
    L = DEPTH
    pt = np.zeros((L, 128, NCOL), np.float32)
    f32 = lambda a: np.asarray(a, dtype=np.float32)
    b_in = f32(inp["b_in"])
    pt[:, :, C_BIN:C_BIN + 64] = b_in.reshape(L, 64, 128).transpose(0, 2, 1)
    cw = f32(inp["a_conv_w"])
    pt[:, :, C_CW:C_CW + 124] = cw.reshape(L, 31, 4, 128).transpose(0, 3, 2, 1).reshape(L, 128, 124)
    for name, c in (("a_conv_b", C_CB), ("a_ln_g", C_ALG), ("a_ln_b", C_ALB)):
        pt[:, :, c:c + 4] = f32(inp[name]).reshape(L, 4, 128).transpose(0, 2, 1)
    for name, c in (("ln1_g", C_L1G), ("ln1_b", C_L1B), ("ln2_g", C_L2G), ("ln2_b", C_L2B), ("ln3_g", C_L3G), ("ln3_b", C_L3B)):
        pt[:, :, c:c + 8] = f32(inp[name]).reshape(L, 8, 128).transpose(0, 2, 1)
    fw = f32(inp["f_conv_w"])
    fb = f32(inp["f_conv_b"])
    for i in range(44):
        f0 = (i * 128) if i < 22 else (FFN + (i - 22) * 128)
        M = min(128, FFN - (i % 22) * 128)
        pt[:, 0:M, C_FW + i * 3:C_FW + i * 3 + 3] = fw[:, :, f0:f0 + M].transpose(0, 2, 1)
        pt[:, 0:M, C_FB + i] = fb[:, f0:f0 + M]
    pt[:, :, C_SUBG] = f32(inp["diff_sub_g"])
    pt[:, :, C_QNG] = np.tile(f32(inp["ax_qn_g"]), (1, 2))
    pt[:, :, C_KNG] = np.tile(f32(inp["ax_kn_g"]), (1, 2))
    brow = np.concatenate([b_in[:, O_BV:O_BV + 512], b_in[:, O_CV:O_CV + 128], b_in[:, O_DV:O_DV + 128]], axis=1)
    brow = np.ascontiguousarray(np.broadcast_to(brow[:, None, :], (L, 128, 768)))
    lam = f32(inp["diff_lam"]).reshape(L, 1, 256)
    sink = np.ascontiguousarray(np.broadcast_to(f32(inp["win_sink"])[:, None, :], (L, 128, 8)))
    return pt, brow, lam, sink


def make_in_maps(inp, core_seqs):
    smax = max(x.shape[0] for cs in core_seqs for (x, m) in cs)
    cm, rope, ohd, ohw = host_consts(smax)
    pt, brow, lam, sink = host_ptab(inp)
    shared = {k: np.ascontiguousarray(np.asarray(inp[k], dtype=np.float32)) for k in
              ("rel_bias", "w_in", "w_branch", "w_mix_out", "w_xq", "w_xkv", "w_xo", "w_up", "w_down")}
    shared.update(ptab=pt, brow=brow, lam=lam, sink=sink, cmat=cm, rope=rope, ohd=ohd, ohw=ohw)
    maps = []
    for cs in core_seqs:
        xT = np.ascontiguousarray(np.concatenate([np.asarray(x, np.float32).T for (x, m) in cs], axis=1))
        mT = np.ascontiguousarray(np.concatenate([np.asarray(m, np.float32).T for (x, m) in cs], axis=1))
        d = dict(shared)
        d["xT"] = xT
        d["memT"] = mT
        maps.append(d)
    return maps


def kernel(**inp):
    xp = np.asarray(inp["x_prompt"], np.float32)
    xs = np.asarray(inp["x_sample"], np.float32)
    mp = np.asarray(inp["mem_prompt"], np.float32)
    ms = np.asarray(inp["mem_sample"], np.float32)
    ncore = 8
    core_seqs = []
    for c in range(ncore):
        core_seqs.append([(xp[2 * c], mp[2 * c]), (xp[2 * c + 1], mp[2 * c + 1]),
                          (xs[2 * c], ms[2 * c]), (xs[2 * c + 1], ms[2 * c + 1])])
    seqs = [x.shape[0] for (x, m) in core_seqs[0]]
    nc = build(seqs)
    maps = make_in_maps(inp, core_seqs)
    res = run_bass_kernel_spmd(nc, maps, core_ids=list(range(ncore)))
    yp = np.empty_like(xp)
    ys = np.empty_like(xs)
    for c in range(ncore):
        yT = res.results[c]["yT"]
        o = 0
        for i, (x, m) in enumerate(core_seqs[c]):
            Sq = x.shape[0]
            blk = yT[:, o:o + Sq].T
            o += Sq
            if i < 2:
                yp[2 * c + i] = blk
            else:
                ys[2 * c + (i - 2)] = blk
    return (yp, ys)
```

```python
import math
from contextlib import ExitStack
import numpy as np
import concourse.bass as bass
import concourse.mybir as mybir
from concourse.bass_utils import run_bass_kernel_spmd

F32 = mybir.dt.float32
BF16 = mybir.dt.bfloat16
AF = mybir.ActivationFunctionType
ALU = mybir.AluOpType
AX = mybir.AxisListType

D = 1024
DEPTH = 2
N_IN = 8192
FFN = 2752
MEM = 256
EPS = 1e-5
ALPHA = (2 * DEPTH) ** 0.25
NEGM = -30000.0
O_UA, O_BQ, O_BK, O_BV, O_CQ, O_CK, O_CV, O_DQ, O_DK, O_DV, O_GL = 0, 1024, 1536, 2048, 2560, 3072, 3200, 3328, 3840, 3968, 4096
C_BIN, C_CW, C_CB, C_ALG, C_ALB = 0, 64, 188, 192, 196
C_L1G, C_L1B, C_L2G, C_L2B, C_L3G, C_L3B = 200, 208, 216, 224, 232, 240
C_FW, C_FB, C_SUBG, C_QNG, C_KNG = 248, 380, 424, 425, 426
NCOL = 432
TD_L = 1280
TD_W = 1152
TW_L = 512
TW_W = 384


class Buf:
    __slots__ = ("w", "r")

    def __init__(self):
        self.w = {}
        self.r = {}


class Tl:
    __slots__ = ("ap", "b")

    def __init__(self, ap, b=None):
        self.ap = ap
        self.b = b if b is not None else Buf()


class Sched:
    ENG = ("pe", "act", "dve", "pool", "sp")

    def __init__(self, nc, es):
        self.nc = nc
        self.sem = {}
        self.cnt = {}
        self.seen = {e: {} for e in self.ENG}
        self.ops = {e: [] for e in self.ENG}
        for e in ("pe", "act", "dve", "pool"):
            self.sem[e] = es.enter_context(nc.semaphore("s_" + e))
            self.cnt[e] = 0
        self.dsem = {}
        self.dval = {}
        self.dnext = {}
        for q, n in (("sp", 24), ("pool", 16), ("act", 8)):
            self.dsem[q] = [es.enter_context(nc.semaphore("d_%s%d" % (q, i))) for i in range(n)]
            self.dval[q] = [0] * n
            self.dnext[q] = 0
        self.allsems = {}
        for e in ("pe", "act", "dve", "pool"):
            self.allsems[id(self.sem[e])] = self.sem[e]
        for q in self.dsem:
            for s in self.dsem[q]:
                self.allsems[id(s)] = s
        self.latest = {}
        self.n_ops = 0

    def _deps(self, eng, reads, writes, own):
        deps = {}
        for b in reads:
            for k, v in b.w.items():
                if k == own and eng == "pe":
                    continue
                if deps.get(k, 0) < v:
                    deps[k] = v
        for b in writes:
            for src in (b.w, b.r):
                for k, v in src.items():
                    if k == own:
                        continue
                    if deps.get(k, 0) < v:
                        deps[k] = v
        return deps

    def _commit(self, eng, deps, fn, key, val, inc, reads, writes):
        seen = self.seen[eng]
        waits = []
        for k, v in deps.items():
            if seen.get(k, 0) < v:
                seen[k] = v
                waits.append((self.allsems[k], v))
        self.ops[eng].append((waits, fn, self.allsems[key], inc))
        self.latest[key] = val
        for b in writes:
            b.w = {key: val}
            b.r = {}
        for b in reads:
            if b.r.get(key, 0) < val:
                b.r[key] = val
        self.n_ops += 1

    def op(self, eng, fn, r=(), w=()):
        reads = [t.b for t in r]
        writes = [t.b for t in w]
        key = id(self.sem[eng])
        deps = self._deps(eng, reads, writes, key)
        self.cnt[eng] += 1
        self._commit(eng, deps, fn, key, self.cnt[eng], 1, reads, writes)

    def dma(self, q, out, in_, r=(), w=()):
        reads = [t.b for t in r]
        writes = [t.b for t in w]
        i = self.dnext[q]
        self.dnext[q] = (i + 1) % len(self.dsem[q])
        s = self.dsem[q][i]
        key = id(s)
        deps = self._deps(q, reads, writes, None)
        prev = self.dval[q][i]
        if prev > 0 and deps.get(key, 0) < prev:
            deps[key] = prev
        self.dval[q][i] = prev + 16
        self._commit(q, deps, lambda e, o=out, a=in_: e.dma_start(out=o, in_=a), key, prev + 16, 16, reads, writes)

    def barrier(self):
        for e in self.ENG:
            seen = self.seen[e]
            waits = []
            for k, v in self.latest.items():
                if seen.get(k, 0) < v:
                    seen[k] = v
                    waits.append((self.allsems[k], v))
            if waits:
                self.ops[e].append((waits, None, None, 0))

    def emit(self):
        nc = self.nc

        def run(e, lst):
            for waits, fn, s, inc in lst:
                for ws, wv in waits:
                    e.wait_ge(ws, wv)
                if fn is not None:
                    fn(e).then_inc(s, inc)

        with nc.Block() as block:
            @block.tensor
            def _(e):
                run(e, self.ops["pe"])

            @block.scalar
            def _(e):
                run(e, self.ops["act"])

            @block.vector
            def _(e):
                run(e, self.ops["dve"])

            @block.gpsimd
            def _(e):
                run(e, self.ops["pool"])

            @block.sync
            def _(e):
                run(e, self.ops["sp"])


class Arena:
    def __init__(self, ap, dtype_bytes):
        self.ap = ap
        self.n = ap.shape[1]
        self.off = 0

    def reset(self):
        self.off = 0

    def take(self, shape):
        n = 1
        for s in shape[1:]:
            n *= s
        assert self.off + n <= self.n, ("arena overflow", self.off, n, self.n)
        v = self.ap[0:shape[0], self.off:self.off + n]
        self.off += n
        if len(shape) == 3:
            v = v.rearrange("p (a b) -> p a b", a=shape[1])
        elif len(shape) == 4:
            v = v.rearrange("p (a b c) -> p a b c", a=shape[1], b=shape[2])
        return Tl(v)


def t5_bucket_np(rel):
    half, max_exact = 16, 8
    n = np.abs(rel)
    nf = np.maximum(n, 1).astype(np.float32)
    large = max_exact + (np.log(nf / np.float32(max_exact)) / np.float32(math.log(128 / max_exact)) * np.float32(half - max_exact)).astype(np.int32)
    large = np.minimum(large, half - 1)
    return np.where(rel > 0, half, 0) + np.where(n < max_exact, n, large)


def build(seqs, debug=False):
    TOK = sum(seqs)
    NSEQ = len(seqs)
    SMAX = max(seqs)
    goff = [sum(seqs[:i]) for i in range(NSEQ)]
    tiles = []
    for s, S in enumerate(seqs):
        for t in range(S // 512):
            tiles.append((s, t * 512, goff[s] + t * 512))
    NT = len(tiles)

    nc = bass.Bass("TRN2", target_bir_lowering=False)

    def din(name, shape, dt=F32):
        return nc.dram_tensor(name, list(shape), dt, kind="ExternalInput").ap()

    def dscr(name, shape, dt):
        kind = "ExternalOutput" if debug else "Internal"
        return Tl(nc.dram_tensor(name, list(shape), dt, kind=kind).ap())

    xT = Tl(din("xT", [D, TOK]))
    memT = din("memT", [D, NSEQ * MEM])
    rel_bias = din("rel_bias", [32, 16])
    w_in = din("w_in", [DEPTH, D, N_IN])
    w_branch = din("w_branch", [DEPTH, 4, 512, D])
    w_mix_out = din("w_mix_out", [DEPTH, D, D])
    w_xq = din("w_xq", [DEPTH, D, D])
    w_xkv = din("w_xkv", [DEPTH, D, 2 * D])
    w_xo = din("w_xo", [DEPTH, D, D])
    w_up = din("w_up", [DEPTH, D, 2 * FFN])
    w_down = din("w_down", [DEPTH, FFN, D])
    ptab_d = din("ptab", [DEPTH, 128, NCOL])
    brow_d = din("brow", [DEPTH, 128, 768])
    lam_d = din("lam", [DEPTH, 1, 256])
    sink_d = din("sink", [DEPTH, 128, 8])
    cmat_d = din("cmat", [128, 512])
    rope_d = din("rope", [2, 128, SMAX])
    ohd_d = din("ohd", [33, TD_L])
    ohw_d = din("ohw", [33, TW_L])
    yT = Tl(nc.dram_tensor("yT", [D, TOK], F32, kind="ExternalOutput").ap())

    QD = dscr("QD", [512, TOK], BF16)
    KD = dscr("KD", [8, 128, TOK], BF16)
    VD = dscr("VD", [TOK, 512], BF16)
    QW = dscr("QW", [512, TOK], BF16)
    KW = dscr("KW", [2, 2, 128, TOK], BF16)
    VW = dscr("VW", [TOK, 2, 128], BF16)
    QA = dscr("QA", [512, TOK], BF16)
    KA = dscr("KA", [2, 2, 128, TOK], BF16)
    VA = dscr("VA", [TOK, 2, 128], BF16)
    OBR = [dscr("O%d" % n, [512, TOK], BF16) for n in range(4)]
    MG = dscr("MG", [D, TOK], BF16)
    ACS = dscr("ACS", [22 * 128, TOK], BF16)
    X2 = dscr("X2", [D, TOK], F32)
    X3 = dscr("X3", [D, TOK], F32)
    MK = dscr("MK", [NSEQ, D, MEM], BF16)
    MV = dscr("MV", [NSEQ, MEM, D], BF16)
    TSD = dscr("TSD", [16, TD_L], F32)
    TSW = dscr("TSW", [16, TW_L], F32)
    BTD = dscr("BTD", [8, 128, TD_W], F32)
    BTW = dscr("BTW", [8, 128, TW_W], F32)

    es = ExitStack()
    with es:
        S = Sched(nc, es)

        def sb(name, shape, dt):
            return es.enter_context(nc.sbuf_tensor("sb_" + name, list(shape), dt))

        WA_t = sb("WA", [128, 24576], BF16)
        WB_t = sb("WB", [128, 24576], BF16)
        WK32_t = sb("WK32", [128, 12288], F32)
        WK16_t = sb("WK16", [128, 28672], BF16)
        ptab = Tl(sb("ptab", [128, NCOL], F32)[:])
        cmat = Tl(sb("cmat", [128, 512], F32)[:])
        cst = Tl(sb("cst", [128, 5 * 128], F32)[:])
        onesb = Tl(sb("onesb", [128, 128], BF16)[:])
        misc = Tl(sb("misc", [128, 16], F32)[:])
        epst = Tl(sb("epst", [128, 1], F32)[:])
        banks = [Tl(es.enter_context(nc.psum_tensor("pb%d" % i, [128, 512], F32))[:]) for i in range(8)]
        WA = Tl(WA_t[:])
        WB = Tl(WB_t[:])
        A32 = Arena(WK32_t[:], 4)
        A16 = Arena(WK16_t[:], 2)

        J_ap = cmat.ap[:, 0:128]
        PERM = cmat.ap[:, 128:256]
        IDN = cmat.ap[:, 256:384]
        ONES_D = cst.ap[:, 0:128]
        ONES_512 = cst.ap[:, 128:256]
        BLK64 = cst.ap[:, 256:384]
        ONES_128 = cst.ap[:, 384:512]
        ONES_1 = cst.ap[:, 512:640]

        def pcol(c):
            return ptab.ap[:, c:c + 1]

        def mm(out, lhsT, rhs, start, stop, r, w):
            S.op("pe", lambda e: e.matmul(out, lhsT=lhsT, rhs=rhs, start=start, stop=stop), r=r, w=w)

        def act(out, in_, func, r, w, bias=None, scale=None):
            kw = {}
            if bias is not None:
                kw["bias"] = bias
            if scale is not None:
                kw["scale"] = scale
            S.op("act", lambda e: e.activation(out=out, in_=in_, func=func, **kw), r=r, w=w)

        def tt(eng, out, in0, in1, op, r, w):
            S.op(eng, lambda e: e.tensor_tensor(out=out, in0=in0, in1=in1, op=op), r=r, w=w)

        def ts(eng, out, in0, s1, s2, op0, op1, r, w):
            if op1 is None:
                S.op(eng, lambda e: e.tensor_scalar(out=out, in0=in0, scalar1=s1, scalar2=None, op0=op0), r=r, w=w)
            else:
                S.op(eng, lambda e: e.tensor_scalar(out=out, in0=in0, scalar1=s1, scalar2=s2, op0=op0, op1=op1), r=r, w=w)

        def stt(eng, out, in0, sc, in1, op0, op1, r, w):
            S.op(eng, lambda e: e.scalar_tensor_tensor(out=out, in0=in0, scalar=sc, in1=in1, op0=op0, op1=op1), r=r, w=w)

        def cp(eng, out, in_, r, w):
            if eng == "act":
                S.op("act", lambda e: e.activation(out=out, in_=in_, func=AF.Copy), r=r, w=w)
            else:
                S.op(eng, lambda e: e.tensor_copy(out=out, in_=in_), r=r, w=w)

        def rsqrt_eps(ot, out, in_, it):
            act(out, in_, AF.Ln, r=(it, epst), w=(ot,), bias=epst.ap[0:out.shape[0], 0:1])
            act(out, out, AF.Exp, r=(ot,), w=(ot,), scale=-0.5)

        def recip(ot, out, in_, it, bias=None):
            if bias is None:
                act(out, in_, AF.Ln, r=(it,), w=(ot,))
            else:
                act(out, in_, AF.Ln, r=(it, misc), w=(ot,), bias=bias)
            act(out, out, AF.Exp, r=(ot,), w=(ot,), scale=-1.0)

        def memset(eng, t, ap, val):
            S.op(eng, lambda e: e.memset(ap, val), r=(), w=(t,))

        def phase():
            S.barrier()
            A32.reset()
            A16.reset()

        def load_x_bf(dst, src, g0, n, col0=0):
            S.dma("pool", dst.ap[:, :, col0:col0 + n], src.ap[:, g0:g0 + n].rearrange("(k p) n -> p k n", p=128), r=(src,), w=(dst,))

        def load_w(arena, view, src):
            S.dma("pool", view, src, r=(), w=(arena,))

        def layer_norm(zc, n, ones_ap, gcol, bcol, stat_banks, tmp, out_bf=None, func=AF.Identity, out_f32=True):
            nch = len(zc)
            b1, b2 = stat_banks
            for c in range(nch):
                mm(b1.ap[:, 0:n], ones_ap, zc[c].ap[:, 0:n], c == 0, c == nch - 1, r=(zc[c], cst), w=(b1,))
            for c in range(nch):
                sq = tmp[c % 2]
                act(sq.ap[:, 0:n], zc[c].ap[:, 0:n], AF.Square, r=(zc[c],), w=(sq,))
                mm(b2.ap[:, 0:n], ones_ap, sq.ap[:, 0:n], c == 0, c == nch - 1, r=(sq, cst), w=(b2,))
            m = tmp[2]
            v = tmp[3]
            cp("act", m.ap[:, 0:n], b1.ap[:, 0:n], r=(b1,), w=(m,))
            tt("dve", v.ap[:, 0:n], b1.ap[:, 0:n], m.ap[:, 0:n], ALU.mult, r=(b1, m), w=(v,))
            tt("dve", v.ap[:, 0:n], b2.ap[:, 0:n], v.ap[:, 0:n], ALU.subtract, r=(b2, v), w=(v,))
            rsqrt_eps(v, v.ap[:, 0:n], v.ap[:, 0:n], v)
            for c in range(nch):
                xc = tmp[4 + (c % 2)]
                tt("pool", xc.ap[:, 0:n], zc[c].ap[:, 0:n], m.ap[:, 0:n], ALU.subtract, r=(zc[c], m), w=(xc,))
                tt("dve", xc.ap[:, 0:n], xc.ap[:, 0:n], v.ap[:, 0:n], ALU.mult, r=(xc, v), w=(xc,))
                if out_f32:
                    act(zc[c].ap[:, 0:n], xc.ap[:, 0:n], func, r=(xc, ptab), w=(zc[c],), bias=pcol(bcol + c), scale=pcol(gcol + c))
                    if out_bf is not None:
                        cp("pool", out_bf.ap[:, c, 0:n], zc[c].ap[:, 0:n], r=(zc[c],), w=(out_bf,))
                else:
                    act(out_bf.ap[:, c, 0:n], xc.ap[:, 0:n], func, r=(xc, ptab), w=(out_bf,), bias=pcol(bcol + c), scale=pcol(gcol + c))

        def chunks(t, n):
            return [Tl(t.ap[:, c, :]) for c in range(n)]

        S.dma("sp", cmat.ap, cmat_d, w=(cmat,))
        memset("dve", cst, cst.ap[:, 0:128], 1.0 / 1024)
        memset("dve", cst, cst.ap[:, 128:256], 1.0 / 512)
        memset("dve", cst, cst.ap[:, 256:384], 0.0)
        memset("dve", cst, cst.ap[0:64, 256:320], 1.0 / 64)
        memset("dve", cst, cst.ap[64:128, 320:384], 1.0 / 64)
        memset("dve", cst, cst.ap[:, 384:512], 1.0 / 128)
        memset("dve", cst, cst.ap[:, 512:640], 1.0)
        memset("pool", onesb, onesb.ap, 1.0)
        memset("pool", epst, epst.ap, EPS)

        phase()
        rb = A32.take([33, 16])
        ohd = A32.take([33, TD_L])
        ohw = A32.take([33, TW_L])
        tsd = A32.take([16, TD_L])
        tsw = A32.take([16, TW_L])
        memset("dve", rb, rb.ap[32:33, :], 1.0)
        S.dma("sp", rb.ap[0:32, :], rel_bias, w=(rb,))
        S.dma("sp", ohd.ap, ohd_d, w=(ohd,))
        S.dma("sp", ohw.ap, ohw_d, w=(ohw,))
        for (oh, tsx, L, dst) in ((ohd, tsd, TD_L, TSD), (ohw, tsw, TW_L, TSW)):
            for i, c0 in enumerate(range(0, L, 512)):
                n = min(512, L - c0)
                bk = banks[i % 4]
                mm(bk.ap[0:16, 0:n], rb.ap[:, :], oh.ap[:, c0:c0 + n], True, True, r=(rb, oh), w=(bk,))
                cp("act", tsx.ap[:, c0:c0 + n], bk.ap[0:16, 0:n], r=(bk,), w=(tsx,))
            S.dma("sp", dst.ap, tsx.ap, r=(tsx,), w=(dst,))
        hk = [A32.take([128, TD_W]) for _ in range(2)]
        wt = [A32.take([128, TD_W]) for _ in range(2)]
        for m in range(16):
            if m < 8:
                src, L, Wd, dst = TSD, TD_L, TD_W, BTD.ap[m]
            else:
                src, L, Wd, dst = TSW, TW_L, TW_W, BTW.ap[m - 8]
            h = hk[m % 2]
            o = wt[m % 2]
            hank = bass.AP(src.ap.tensor, m * L, [[1, 128], [1, Wd]])
            S.dma("sp", h.ap[:, 0:Wd], hank, r=(src,), w=(h,))
            for i, c0 in enumerate(range(0, Wd, 512)):
                n = min(512, Wd - c0)
                bk = banks[4 + (i % 4)]
                mm(bk.ap[:, 0:n], J_ap, h.ap[:, c0:c0 + n], True, True, r=(cmat, h), w=(bk,))
                cp("act" if i % 2 == 0 else "dve", o.ap[:, c0:c0 + n], bk.ap[:, 0:n], r=(bk,), w=(o,))
            S.dma("sp", dst, o.ap[:, 0:Wd], r=(o,), w=(BTD if m < 8 else BTW,))

        phase()
        zt = A16.take([64, 2048])
        memset("pool", zt, zt.ap, 0.0)
        for c0 in range(0, TOK, 2048):
            n = min(2048, TOK - c0)
            for m in range(8):
                o = 1 - (m % 2)
                S.dma("sp", KD.ap[m, o * 64:(o + 1) * 64, c0:c0 + n], zt.ap[:, 0:n], r=(zt,), w=(KD,))
            for KX in (KW, KA):
                for kv in range(2):
                    for v in range(2):
                        o = 1 - v
                        S.dma("sp", KX.ap[kv, v, o * 64:(o + 1) * 64, c0:c0 + n], zt.ap[:, 0:n], r=(zt,), w=(KX,))

        for l in range(DEPTH):
            XIN = xT if l == 0 else X3
            XOUT = X3 if l < DEPTH - 1 else yT
            lam_init = 0.8 - 0.6 * math.exp(-0.3 * l)

            phase()
            S.dma("sp", ptab.ap, ptab_d[l], w=(ptab,))
            lamt = A32.take([1, 256])
            lt2 = A32.take([1, 128])
            lt3 = A32.take([1, 4])
            skt = A32.take([128, 8])
            S.dma("sp", lamt.ap, lam_d[l], w=(lamt,))
            S.dma("sp", skt.ap, sink_d[l], w=(skt,))
            lv = lamt.ap.rearrange("p (a b d) -> p a b d", a=2, b=2)
            tt("dve", lt2.ap.rearrange("p (a d) -> p a d", a=2), lv[:, :, 0, :], lv[:, :, 1, :], ALU.mult, r=(lamt,), w=(lt2,))
            S.op("dve", lambda e, o=lt3.ap[:, 0:2], i=lt2.ap.rearrange("p (a d) -> p a d", a=2): e.reduce_sum(out=o, in_=i, axis=AX.X), r=(lt2,), w=(lt3,))
            act(lt3.ap[:, 0:2], lt3.ap[:, 0:2], AF.Exp, r=(lt3,), w=(lt3,))
            tt("dve", lt3.ap[:, 2:3], lt3.ap[:, 0:1], lt3.ap[:, 1:2], ALU.subtract, r=(lt3,), w=(lt3,))
            ts("dve", lt3.ap[:, 3:4], lt3.ap[:, 2:3], lam_init, -1.0, ALU.add, ALU.mult, r=(lt3,), w=(lt3,))
            mm(banks[0].ap[:, 0:1], ONES_1[0:1, :], lt3.ap[:, 3:4], True, True, r=(cst, lt3), w=(banks[0],))
            cp("act", misc.ap[:, 0:1], banks[0].ap[:, 0:1], r=(banks[0],), w=(misc,))
            ts("dve", misc.ap[:, 1:2], pcol(C_SUBG), 1.0 - lam_init, None, ALU.mult, None, r=(ptab,), w=(misc,))
            act(misc.ap[:, 2:10], skt.ap, AF.Exp, r=(skt,), w=(misc,))

            phase()
            WAv = WA.ap.rearrange("p (k n) -> p k n", k=8)
            for k in range(8):
                load_w(WA, WAv[:, k, :], w_in[l, k * 128:(k + 1) * 128, 1024:4096])
            brow = A32.take([128, 768])
            S.dma("sp", brow.ap, brow_d[l], w=(brow,))
            xb2 = [A16.take([128, 8, 512]) for _ in range(2)]
            cs2 = [A32.take([128, 2, 512]) for _ in range(2)]
            stg = [A16.take([128, 4, 512]) for _ in range(3)]
            stgv = [A16.take([128, 4, 512]) for _ in range(2)]
            stgw = [A16.take([128, 4, 2, 2, 128]) if False else A16.take([128, 4, 512]) for _ in range(2)]
            tq = [A32.take([128, 512]) for _ in range(8)]
            for sw in stgw:
                memset("pool", sw, sw.ap, 1.0)
            fm_groups = [(O_BQ, 4, QD, 0), (O_BK, 4, KD, 0), (O_CQ, 4, QW, 0), (O_CK, 1, KW, 0),
                         (O_DQ, 4, QA, 1), (O_DK, 1, KA, 2)]
            sgi = 0
            ev = 0
            pbi = 0
            def pa_load(ti):
                s, l0, g0 = tiles[ti]
                load_x_bf(xb2[ti % 2], XIN, g0, 512)
                S.dma("sp", cs2[ti % 2].ap, rope_d[:, :, l0:l0 + 512].rearrange("a p n -> p a n"), w=(cs2[ti % 2],))

            pa_load(0)
            for ti, (s, l0, g0) in enumerate(tiles):
                if ti + 1 < NT:
                    pa_load(ti + 1)
                xb = xb2[ti % 2]
                cs = cs2[ti % 2]
                for (off, nchk, dst, kind) in fm_groups:
                    st = stg[sgi % 3]
                    sgi += 1
                    for c in range(nchk):
                        bk = banks[pbi % 4]
                        pbi += 1
                        wc = off - 1024 + c * 128
                        for k in range(8):
                            mm(bk.ap, WAv[:, k, wc:wc + 128], xb.ap[:, k, :], k == 0, k == 7, r=(WA, xb), w=(bk,))
                        bcolap = pcol(C_BIN + (off + c * 128) // 128)
                        if kind == 0:
                            if ev % 2 == 0:
                                act(st.ap[:, c, :], bk.ap, AF.Identity, r=(bk, ptab), w=(st,), bias=bcolap)
                            else:
                                ts("dve", st.ap[:, c, :], bk.ap, bcolap, None, ALU.add, None, r=(bk, ptab), w=(st,))
                            ev += 1
                        else:
                            q32, sq, rr, qn, aa, bb = tq[0], tq[1], tq[2], tq[3], tq[4], tq[5]
                            act(q32.ap, bk.ap, AF.Identity, r=(bk, ptab), w=(q32,), bias=bcolap)
                            act(sq.ap, bk.ap, AF.Square, r=(bk, ptab), w=(sq,), bias=bcolap)
                            b4 = banks[4]
                            b5 = banks[5]
                            mm(b4.ap, BLK64, sq.ap, True, True, r=(cst, sq), w=(b4,))
                            rsqrt_eps(rr, rr.ap, b4.ap, b4)
                            stt("dve", qn.ap, q32.ap, pcol(C_QNG if kind == 1 else C_KNG), rr.ap, ALU.mult, ALU.mult, r=(q32, rr, ptab), w=(qn,))
                            mm(b5.ap, PERM, qn.ap, True, True, r=(cmat, qn), w=(b5,))
                            tt("pool", aa.ap, qn.ap, cs.ap[:, 0, :], ALU.mult, r=(qn, cs), w=(aa,))
                            tt("dve", bb.ap, b5.ap, cs.ap[:, 1, :], ALU.mult, r=(b5, cs), w=(bb,))
                            tt("pool", st.ap[:, c, :], aa.ap, bb.ap, ALU.add, r=(aa, bb), w=(st,))
                    if dst is KW or dst is KA:
                        for kv in range(2):
                            for v in range(2):
                                S.dma("sp", dst.ap[kv, v, v * 64:(v + 1) * 64, g0:g0 + 512], st.ap[kv * 64:(kv + 1) * 64, 0, :], r=(st,), w=(dst,))
                    elif dst is KD:
                        for c in range(4):
                            for m2 in range(2):
                                S.dma("sp", KD.ap[2 * c + m2, m2 * 64:(m2 + 1) * 64, g0:g0 + 512], st.ap[m2 * 64:(m2 + 1) * 64, c, :], r=(st,), w=(KD,))
                    else:
                        S.dma("sp", dst.ap[:, g0:g0 + 512].rearrange("(c p) n -> p c n", p=128), st.ap[:, 0:nchk, :], r=(st,), w=(dst,))
                sv = stgv[ti % 2]
                sw = stgw[ti % 2]
                swv = sw.ap.rearrange("p j (a b d) -> p j a b d", a=2, b=2)
                for j in range(4):
                    b6 = banks[6]
                    b7 = banks[7]
                    for k in range(8):
                        mm(b6.ap, xb.ap[:, k, j * 128:(j + 1) * 128], WAv[:, k, O_BV - 1024:O_BV - 1024 + 512], k == 0, k == 7, r=(WA, xb), w=(b6,))
                    for k in range(8):
                        mm(b7.ap[:, 0:128], xb.ap[:, k, j * 128:(j + 1) * 128], WAv[:, k, O_CV - 1024:O_CV - 1024 + 128], k == 0, k == 7, r=(WA, xb), w=(b7,))
                    for k in range(8):
                        mm(b7.ap[:, 128:256], xb.ap[:, k, j * 128:(j + 1) * 128], WAv[:, k, O_DV - 1024:O_DV - 1024 + 128], k == 0, k == 7, r=(WA, xb), w=(b7,))
                    tt("dve", sv.ap[:, j, :], b6.ap, brow.ap[:, 0:512], ALU.add, r=(b6, brow), w=(sv,))
                    swj = sw.ap[:, j, :].rearrange("p (a d) -> p a d", a=4)
                    tt("dve", swj[:, :, 0:64], b7.ap[:, 0:256].rearrange("p (a d) -> p a d", a=4), brow.ap[:, 512:768].rearrange("p (a d) -> p a d", a=4), ALU.add, r=(b7, brow), w=(sw,))
                S.dma("sp", VD.ap[g0:g0 + 512, :].rearrange("(j p) f -> p j f", p=128), sv.ap, r=(sv,), w=(VD,))
                S.dma("sp", VW.ap[g0:g0 + 512].rearrange("(j p) a d -> p j (a d)", p=128), sw.ap[:, :, 0:256], r=(sw,), w=(VW,))
                S.dma("sp", VA.ap[g0:g0 + 512].rearrange("(j p) a d -> p j (a d)", p=128), sw.ap[:, :, 256:512], r=(sw,), w=(VA,))

            phase()
            WBv = WB.ap[:, 0:8192].rearrange("p (k n) -> p k n", k=8)
            Dg = WB.ap[:, 8192:8192 + 124 * 128].rearrange("p (j n) -> p j n", j=124)
            for k in range(8):
                load_w(WB, WBv[:, k, :], w_in[l, k * 128:(k + 1) * 128, 0:1024])
            for cj in range(124):
                ts("dve", Dg[:, cj, :], IDN, pcol(C_CW + cj), None, ALU.mult, None, r=(cmat, ptab), w=(WB,))
            xw2 = [A16.take([128, 8, 544]) for _ in range(2)]
            hb2 = [A16.take([128, 4, 544]) for _ in range(2)]
            sg2 = [A32.take([128, 544]) for _ in range(2)]
            acc = chunks(A32.take([128, 4, 512]), 4)
            tmpb = [A32.take([128, 512]) for _ in range(6)]
            sto = [A16.take([128, 4, 512]) for _ in range(2)]

            def pb_geom(ti):
                s, l0, g0 = tiles[ti]
                lo = max(l0 - 15, 0)
                hi = min(l0 + 527, seqs[s])
                return lo, hi - lo, lo - (l0 - 15)

            def pb_load(ti):
                s, l0, g0 = tiles[ti]
                lo, n, c0 = pb_geom(ti)
                load_x_bf(xw2[ti % 2], XIN, g0 - l0 + lo, n, col0=c0)

            def pb_stage1(ti):
                lo, n, c0 = pb_geom(ti)
                xw = xw2[ti % 2]
                hb = hb2[ti % 2]
                if c0 > 0:
                    memset("pool", hb, hb.ap[:, :, 0:c0], 0.0)
                if c0 + n < 542:
                    memset("pool", hb, hb.ap[:, :, c0 + n:542], 0.0)
                n1 = min(n, 512)
                n2 = n - n1
                for c in range(4):
                    bA, bG = banks[(c % 2) * 2], banks[(c % 2) * 2 + 1]
                    b4 = banks[4]
                    sg = sg2[c % 2]
                    for (bk, wc) in ((bA, c * 128), (bG, 512 + c * 128)):
                        for k in range(8):
                            mm(bk.ap[:, 0:n1], WBv[:, k, wc:wc + 128], xw.ap[:, k, c0:c0 + n1], k == 0, k == 7, r=(WB, xw), w=(bk,))
                    if n2 > 0:
                        for (co, wc) in ((0, c * 128), (64, 512 + c * 128)):
                            for k in range(8):
                                mm(b4.ap[:, co:co + n2], WBv[:, k, wc:wc + 128], xw.ap[:, k, c0 + n1:c0 + n], k == 0, k == 7, r=(WB, xw), w=(b4,))
                    act(sg.ap[:, 0:n1], bG.ap[:, 0:n1], AF.Sigmoid, r=(bG, ptab), w=(sg,), bias=pcol(C_BIN + 4 + c))
                    stt("dve", hb.ap[:, c, c0:c0 + n1], bA.ap[:, 0:n1], pcol(C_BIN + c), sg.ap[:, 0:n1], ALU.add, ALU.mult, r=(bA, sg, ptab), w=(hb,))
                    if n2 > 0:
                        act(sg.ap[:, 512:512 + n2], b4.ap[:, 64:64 + n2], AF.Sigmoid, r=(b4, ptab), w=(sg,), bias=pcol(C_BIN + 4 + c))
                        stt("dve", hb.ap[:, c, c0 + n1:c0 + n], b4.ap[:, 0:n2], pcol(C_BIN + c), sg.ap[:, 512:512 + n2], ALU.add, ALU.mult, r=(b4, sg, ptab), w=(hb,))

            def pb_stage2(ti):
                s, l0, g0 = tiles[ti]
                hb = hb2[ti % 2]
                for c in range(4):
                    bk = banks[5 + (c % 2)]
                    for j in range(31):
                        mm(bk.ap, Dg[:, c * 31 + j, :], hb.ap[:, c, j:j + 512], j == 0, j == 30, r=(WB, hb), w=(bk,))
                    if c % 2 == 0:
                        act(acc[c].ap, bk.ap, AF.Identity, r=(bk, ptab), w=(acc[c],), bias=pcol(C_CB + c))
                    else:
                        ts("dve", acc[c].ap, bk.ap, pcol(C_CB + c), None, ALU.add, None, r=(bk, ptab), w=(acc[c],))
                so = sto[ti % 2]
                layer_norm(acc, 512, ONES_512, C_ALG, C_ALB, (banks[7], banks[5]), tmpb, out_bf=so, func=AF.Silu, out_f32=False)
                S.dma("sp", OBR[0].ap[:, g0:g0 + 512].rearrange("(c p) n -> p c n", p=128), so.ap, r=(so,), w=(OBR[0],))

            pb_load(0)
            if NT > 1:
                pb_load(1)
            pb_stage1(0)
            for ti in range(NT):
                if ti + 1 < NT:
                    pb_stage1(ti + 1)
                if ti + 2 < NT:
                    pb_load(ti + 2)
                pb_stage2(ti)

            phase()
            kt2 = [A16.take([128, 2, SMAX]) for _ in range(2)]
            vv2 = [A16.take([128, SMAX // 128, 128]) for _ in range(2)]
            qt2 = [A16.take([128, 512]) for _ in range(2)]
            pt = [A16.take([128, 512]) for _ in range(4)]
            ost = [A16.take([128, 512]) for _ in range(2)]
            bt2 = [A32.take([128, 2, TD_W]) for _ in range(2)]
            tn = [A32.take([128, 512]) for _ in range(3)]
            fz = [A32.take([128, 512]) for _ in range(8)]
            hi_ = 0
            qi_ = 0
            pi_ = 0
            ni_ = 0
            dpend = [None]
            for s, Sq in enumerate(seqs):
                G0 = goff[s]
                NB = Sq // 128
                for h in range(4):
                    kt = kt2[hi_ % 2]
                    vv = vv2[hi_ % 2]
                    bt = bt2[hi_ % 2]
                    hi_ += 1
                    S.dma("sp", kt.ap[:, :, 0:Sq], KD.ap[2 * h:2 * h + 2, :, G0:G0 + Sq].rearrange("m p s -> p m s"), r=(KD,), w=(kt,))
                    S.dma("sp", vv.ap[:, 0:NB, :], VD.ap[G0:G0 + Sq, h * 128:(h + 1) * 128].rearrange("(b p) e -> p b e", p=128), r=(VD,), w=(vv,))
                    S.dma("sp", bt.ap, BTD.ap[2 * h:2 * h + 2].rearrange("m p w -> p m w"), r=(BTD,), w=(bt,))
                    for qt_i in range(Sq // 512):
                        q0 = qt_i * 512
                        qt = qt2[qi_ % 2]
                        qi_ += 1
                        S.dma("sp", qt.ap, QD.ap[h * 128:(h + 1) * 128, G0 + q0:G0 + q0 + 512], r=(QD,), w=(qt,))
                        items = [(kb, m2) for kb in range(NB) for m2 in range(2)]
                        acc_o = (banks[3], banks[4])
                        acc_s = (banks[5], banks[6])

                        sbanks = (banks[0], banks[1], banks[2], banks[7])

                        def score(it, idx):
                            kb, m2 = it
                            bk = sbanks[idx % 4]
                            mm(bk.ap, kt.ap[:, m2, kb * 128:(kb + 1) * 128], qt.ap, True, True, r=(kt, qt), w=(bk,))

                        LOOK = 3
                        for i in range(min(LOOK, len(items))):
                            score(items[i], i)
                        if dpend[0] is not None:
                            dpend[0]()
                            dpend[0] = None
                        for idx, (kb, m2) in enumerate(items):
                            bk = sbanks[idx % 4]
                            p = pt[pi_ % 4]
                            pi_ += 1
                            d = kb * 128 - q0
                            if d >= 602:
                                act(p.ap, bk.ap, AF.Exp, r=(bk, bt), w=(p,), bias=bt.ap[:, m2, 0:1], scale=0.125)
                            elif d <= -218:
                                act(p.ap, bk.ap, AF.Exp, r=(bk, bt), w=(p,), bias=bt.ap[:, m2, TD_W - 1:TD_W], scale=0.125)
                            else:
                                t_ = tn[ni_ % 3]
                                ni_ += 1
                                j0 = q0 - kb * 128 + 512
                                stt("dve", t_.ap, bk.ap, 0.125, bt.ap[:, m2, j0:j0 + 512], ALU.mult, ALU.add, r=(bk, bt), w=(t_,))
                                act(p.ap, t_.ap, AF.Exp, r=(t_,), w=(p,))
                            if idx + LOOK < len(items):
                                score(items[idx + LOOK], idx + LOOK)
                            mm(acc_o[m2].ap, vv.ap[:, kb, :], p.ap, kb == 0, kb == NB - 1, r=(vv, p), w=(acc_o[m2],))
                            mm(acc_s[m2].ap, onesb.ap, p.ap, kb == 0, kb == NB - 1, r=(onesb, p), w=(acc_s[m2],))
                        r0, r1, t0, t1, oo, sq, rr = fz[0], fz[1], fz[2], fz[3], fz[4], fz[5], fz[6]
                        cp("act", r0.ap, acc_s[0].ap, r=(acc_s[0],), w=(r0,))
                        cp("dve", t0.ap, acc_o[0].ap, r=(acc_o[0],), w=(t0,))
                        cp("act", r1.ap, acc_s[1].ap, r=(acc_s[1],), w=(r1,))
                        cp("dve", t1.ap, acc_o[1].ap, r=(acc_o[1],), w=(t1,))

                        def dfin(os_=ost[qi_ % 2], h=h, G0=G0, q0=q0):
                            recip(r0, r0.ap, r0.ap, r0)
                            recip(r1, r1.ap, r1.ap, r1)
                            tt("pool", t0.ap, t0.ap, r0.ap, ALU.mult, r=(t0, r0), w=(t0,))
                            tt("pool", t1.ap, t1.ap, r1.ap, ALU.mult, r=(t1, r1), w=(t1,))
                            stt("dve", oo.ap, t1.ap, misc.ap[:, 0:1], t0.ap, ALU.mult, ALU.add, r=(t0, t1, misc), w=(oo,))
                            act(sq.ap, oo.ap, AF.Square, r=(oo,), w=(sq,))
                            b7 = banks[7]
                            mm(b7.ap, ONES_128, sq.ap, True, True, r=(cst, sq), w=(b7,))
                            rsqrt_eps(rr, rr.ap, b7.ap, b7)
                            stt("dve", os_.ap, oo.ap, misc.ap[:, 1:2], rr.ap, ALU.mult, ALU.mult, r=(oo, rr, misc), w=(os_,))
                            S.dma("sp", OBR[1].ap[h * 128:(h + 1) * 128, G0 + q0:G0 + q0 + 512], os_.ap, r=(os_,), w=(OBR[1],))
                        dpend[0] = dfin
            if dpend[0] is not None:
                dpend[0]()
                dpend[0] = None

            for which in ("win", "ax"):
                phase()
                QX, KX, VX, OX = (QW, KW, VW, OBR[2]) if which == "win" else (QA, KA, VA, OBR[3])
                kt2 = [A16.take([128, 2, SMAX]) for _ in range(2)]
                vv2 = [A16.take([128, SMAX // 128, 128]) for _ in range(2)]
                qt2 = [A16.take([128, 512]) for _ in range(3)]
                pt = [A16.take([128, 512]) for _ in range(3)]
                ost = [A16.take([128, 512]) for _ in range(2)]
                btw = [A32.take([128, 4, TW_W]) for _ in range(2)]
                tn = [A32.take([128, 384]) for _ in range(3)]
                rc = [A32.take([128, 512]) for _ in range(2)]
                rs = [A32.take([128, 512]) for _ in range(2)]
                hi_ = qi_ = pi_ = ni_ = ai_ = sci_ = 0
                pending = [None]

                def flush():
                    if pending[0] is not None:
                        f = pending[0]
                        pending[0] = None
                        f()

                for s, Sq in enumerate(seqs):
                    G0 = goff[s]
                    NB = Sq // 128
                    for kv in range(2):
                        kt = kt2[hi_ % 2]
                        vv = vv2[hi_ % 2]
                        bt = btw[hi_ % 2]
                        hi_ += 1
                        S.dma("sp", kt.ap[:, :, 0:Sq], KX.ap[kv, :, :, G0:G0 + Sq].rearrange("v p s -> p v s"), r=(KX,), w=(kt,))
                        S.dma("sp", vv.ap[:, 0:NB, :], VX.ap[G0:G0 + Sq, kv, :].rearrange("(b p) e -> p b e", p=128), r=(VX,), w=(vv,))
                        if which == "win":
                            S.dma("sp", bt.ap, BTW.ap[kv * 4:kv * 4 + 4].rearrange("m p w -> p m w"), r=(BTW,), w=(bt,))
                        for g in range(4):
                            h = kv * 4 + g
                            vsel = h % 2
                            for qt_i in range(Sq // 512):
                                q0 = qt_i * 512
                                qt = qt2[qi_ % 3]
                                qi_ += 1
                                S.dma("sp", qt.ap, QX.ap[(h // 2) * 128:(h // 2) * 128 + 128, G0 + q0:G0 + q0 + 512], r=(QX,), w=(qt,))
                                accb = banks[4 + (ai_ % 2)]
                                b7 = banks[6 + (ai_ % 2)]
                                rcx = rc[ai_ % 2]
                                rsx = rs[ai_ % 2]
                                os_ = ost[ai_ % 2]
                                ai_ += 1
                                if which == "ax":
                                    def score(kb, kt=kt, qt=qt, vsel=vsel):
                                        nonlocal sci_
                                        bk = banks[sci_ % 4]
                                        sci_ += 1
                                        mm(bk.ap, kt.ap[:, vsel, kb * 128:(kb + 1) * 128], qt.ap, True, True, r=(kt, qt), w=(bk,))
                                        return bk
                                    LOOK = 3
                                    sb_ = [score(i) for i in range(min(LOOK, NB))]
                                    flush()
                                    for kb in range(NB):
                                        bk = sb_[kb]
                                        p = pt[pi_ % 3]
                                        pi_ += 1
                                        act(p.ap, bk.ap, AF.Exp, r=(bk,), w=(p,), scale=0.125)
                                        if kb + LOOK < NB:
                                            sb_.append(score(kb + LOOK))
                                        mm(accb.ap, vv.ap[:, kb, :], p.ap, kb == 0, kb == NB - 1, r=(vv, p), w=(accb,))
                                else:
                                    def wscore(qbl, kt=kt, qt=qt, vsel=vsel, qt_i=qt_i, NB=NB):
                                        nonlocal sci_
                                        qb = qt_i * 4 + qbl
                                        kbs = [kb for kb in (qb - 1, qb, qb + 1) if 0 <= kb < NB]
                                        bk = banks[sci_ % 4]
                                        sci_ += 1
                                        slots = []
                                        for kb in kbs:
                                            sl = 2 - (kb - qb + 1)
                                            slots.append(sl)
                                            mm(bk.ap[:, sl * 128:(sl + 1) * 128], kt.ap[:, vsel, kb * 128:(kb + 1) * 128], qt.ap[:, qbl * 128:(qbl + 1) * 128], True, True, r=(kt, qt), w=(bk,))
                                        return bk, kbs, slots
                                    nxt = wscore(0)
                                    flush()
                                    for qbl in range(4):
                                        bk, kbs, slots = nxt
                                        c_lo, c_hi = min(slots) * 128, (max(slots) + 1) * 128
                                        t_ = tn[ni_ % 3]
                                        ni_ += 1
                                        p = pt[pi_ % 3]
                                        pi_ += 1
                                        stt("dve", t_.ap[:, c_lo:c_hi], bk.ap[:, c_lo:c_hi], 0.125, bt.ap[:, g, c_lo:c_hi], ALU.mult, ALU.add, r=(bk, bt), w=(t_,))
                                        act(p.ap[:, c_lo:c_hi], t_.ap[:, c_lo:c_hi], AF.Exp, r=(t_,), w=(p,))
                                        if qbl + 1 < 4:
                                            nxt = wscore(qbl + 1)
                                        for i, kb in enumerate(kbs):
                                            sl = slots[i]
                                            mm(accb.ap[:, qbl * 128:(qbl + 1) * 128], vv.ap[:, kb, :], p.ap[:, sl * 128:(sl + 1) * 128], i == 0, i == len(kbs) - 1, r=(vv, p), w=(accb,))

                                def fin(accb=accb, b7=b7, rcx=rcx, rsx=rsx, os_=os_, h=h, G0=G0, q0=q0):
                                    if which == "win":
                                        recip(rcx, rcx.ap[64:128, :], accb.ap[64:128, :], accb, bias=misc.ap[64:128, 2 + h:3 + h])
                                    else:
                                        recip(rcx, rcx.ap[64:128, :], accb.ap[64:128, :], accb)
                                    mm(b7.ap[0:64, :], IDN[64:128, 64:128], rcx.ap[64:128, :], True, True, r=(cmat, rcx), w=(b7,))
                                    cp("act", rsx.ap[0:64, :], b7.ap[0:64, :], r=(b7,), w=(rsx,))
                                    tt("dve", os_.ap[0:64, :], accb.ap[0:64, :], rsx.ap[0:64, :], ALU.mult, r=(accb, rsx), w=(os_,))
                                    S.dma("sp", OX.ap[h * 64:(h + 1) * 64, G0 + q0:G0 + q0 + 512], os_.ap[0:64, :], r=(os_,), w=(OX,))
                                pending[0] = fin
                flush()

            for hf in range(2):
                phase()
                AR = WA if hf == 0 else WB
                Wg = AR.ap[:, 0:16384].rearrange("p (k n c) -> p k n c", k=8, n=4)
                Wb = AR.ap[:, 16384:24576].rearrange("p (n k c) -> p n k c", n=4, k=4)
                for n in range(4):
                    c0 = O_GL + n * 1024 + hf * 512
                    load_w(AR, Wg[:, :, n, :], w_in[l, :, c0:c0 + 512].rearrange("(k p) c -> p k c", p=128))
                    load_w(AR, Wb[:, n, :, :], w_branch[l, n, :, hf * 512:(hf + 1) * 512].rearrange("(k p) c -> p k c", p=128))
                xb2 = [A16.take([128, 8, 512]) for _ in range(2)]
                ob2_ = [[A16.take([128, 4, 512]) for _ in range(4)] for _ in range(2)]
                stg = [A16.take([128, 4, 512]) for _ in range(2)]
                sgt = [A32.take([128, 512]) for _ in range(3)]
                tmt = [A32.take([128, 512]) for _ in range(3)]
                mgt = [A32.take([128, 512]) for _ in range(2)]
                gi = 0
                def pm1_load(ti):
                    g0_ = tiles[ti][2]
                    load_x_bf(xb2[ti % 2], XIN, g0_, 512)
                    for n in range(4):
                        S.dma("sp", ob2_[ti % 2][n].ap, OBR[n].ap[:, g0_:g0_ + 512].rearrange("(k p) n -> p k n", p=128), r=(OBR[n],), w=(ob2_[ti % 2][n],))

                pm1_load(0)
                for ti, (s, l0, g0) in enumerate(tiles):
                    xb = xb2[ti % 2]
                    ob = ob2_[ti % 2]
                    if ti + 1 < NT:
                        pm1_load(ti + 1)
                    st = stg[ti % 2]
                    for cc in range(4):
                        mg = mgt[cc % 2]
                        for n in range(4):
                            bG = banks[(gi % 2) * 2]
                            bP = banks[(gi % 2) * 2 + 1]
                            sg = sgt[gi % 3]
                            tm = tmt[gi % 3]
                            gi += 1
                            for k in range(8):
                                mm(bG.ap, Wg[:, k, n, cc * 128:(cc + 1) * 128], xb.ap[:, k, :], k == 0, k == 7, r=(AR, xb), w=(bG,))
                            for k in range(4):
                                mm(bP.ap, Wb[:, n, k, cc * 128:(cc + 1) * 128], ob[n].ap[:, k, :], k == 0, k == 3, r=(AR, ob[n]), w=(bP,))
                            gcol = pcol(C_BIN + (O_GL + n * 1024 + hf * 512 + cc * 128) // 128)
                            act(sg.ap, bG.ap, AF.Sigmoid, r=(bG, ptab), w=(sg,), bias=gcol)
                            if n == 0:
                                tt("dve", mg.ap, bP.ap, sg.ap, ALU.mult, r=(bP, sg), w=(mg,))
                            else:
                                tt("dve", tm.ap, bP.ap, sg.ap, ALU.mult, r=(bP, sg), w=(tm,))
                                if n < 3:
                                    tt("pool", mg.ap, mg.ap, tm.ap, ALU.add, r=(mg, tm), w=(mg,))
                                else:
                                    tt("pool", st.ap[:, cc, :], mg.ap, tm.ap, ALU.add, r=(mg, tm), w=(st,))
                    S.dma("sp", MG.ap[hf * 512:(hf + 1) * 512, g0:g0 + 512].rearrange("(c p) n -> p c n", p=128), st.ap, r=(st,), w=(MG,))

            phase()
            Wkv = WA.ap[:, 0:16384].rearrange("p (k n) -> p k n", k=8)
            for k in range(8):
                load_w(WA, Wkv[:, k, :], w_xkv[l, k * 128:(k + 1) * 128, :])
            mt2 = [A16.take([128, 8, MEM]) for _ in range(2)]
            mks = [A16.take([128, 8, MEM]) for _ in range(2)]
            mvs = [A16.take([128, 2, D]) for _ in range(2)]
            pbi = 0
            for s in range(NSEQ):
                mt = mt2[s % 2]
                S.dma("pool", mt.ap, memT[:, s * MEM:(s + 1) * MEM].rearrange("(k p) n -> p k n", p=128), r=(), w=(mt,))
                mk = mks[s % 2]
                mv = mvs[s % 2]
                for c in range(8):
                    bk = banks[pbi % 4]
                    pbi += 1
                    for k in range(8):
                        mm(bk.ap[:, 0:MEM], Wkv[:, k, c * 128:(c + 1) * 128], mt.ap[:, k, :], k == 0, k == 7, r=(WA, mt), w=(bk,))
                    cp("act" if c % 2 == 0 else "dve", mk.ap[:, c, :], bk.ap[:, 0:MEM], r=(bk,), w=(mk,))
                for j in range(2):
                    for hh in range(2):
                        bk = banks[pbi % 4]
                        pbi += 1
                        for k in range(8):
                            mm(bk.ap, mt.ap[:, k, j * 128:(j + 1) * 128], Wkv[:, k, 1024 + hh * 512:1024 + (hh + 1) * 512], k == 0, k == 7, r=(WA, mt), w=(bk,))
                        cp("act" if hh == 0 else "dve", mv.ap[:, j, hh * 512:(hh + 1) * 512], bk.ap, r=(bk,), w=(mv,))
                S.dma("sp", MK.ap[s].rearrange("(c p) n -> p c n", p=128), mk.ap, r=(mk,), w=(MK,))
                S.dma("sp", MV.ap[s].rearrange("(j p) f -> p j f", p=128), mv.ap, r=(mv,), w=(MV,))

            phase()
            W3 = WB.ap.rearrange("p (w k n) -> p w k n", w=3, k=8)
            for wi, wsrc in enumerate((w_mix_out, w_xq, w_xo)):
                for k in range(8):
                    load_w(WB, W3[:, wi, k, :], wsrc[l, k * 128:(k + 1) * 128, :])
            xs2 = [A32.take([128, 8, 512]) for _ in range(2)]
            xsc2 = [chunks(t, 8) for t in xs2]
            tmpm = [A32.take([128, 512]) for _ in range(6)]
            rcp = A32.take([128, 512])
            mgb2 = [A16.take([128, 8, 512]) for _ in range(2)]
            x1b = A16.take([128, 8, 512])
            qb = A16.take([128, 8, 512])
            ob2 = A16.take([128, 8, 512])
            pt = [A16.take([128, 512]) for _ in range(4)]
            mkt = A16.take([128, 8, MEM])
            mvt = A16.take([128, 2, D])
            cur_s = -1
            pbi = 0
            pi_ = 0

            def pm2_load(ti):
                g0 = tiles[ti][2]
                S.dma("sp", xs2[ti % 2].ap, XIN.ap[:, g0:g0 + 512].rearrange("(k p) n -> p k n", p=128), r=(XIN,), w=tuple(xsc2[ti % 2]))
                S.dma("sp", mgb2[ti % 2].ap, MG.ap[:, g0:g0 + 512].rearrange("(k p) n -> p k n", p=128), r=(MG,), w=(mgb2[ti % 2],))

            def dense_res(wi, src, xsc):
                nonlocal pbi
                for c in range(8):
                    bk = banks[pbi % 3]
                    pbi += 1
                    for k in range(8):
                        mm(bk.ap, W3[:, wi, k, c * 128:(c + 1) * 128], src.ap[:, k, :], k == 0, k == 7, r=(WB, src), w=(bk,))
                    stt("dve", xsc[c].ap, xsc[c].ap, ALPHA, bk.ap, ALU.mult, ALU.add, r=(xsc[c], bk), w=(xsc[c],))

            pm2_load(0)
            dense_res(0, mgb2[0], xsc2[0])
            for ti, (s, l0, g0) in enumerate(tiles):
                if s != cur_s:
                    cur_s = s
                    S.dma("sp", mkt.ap, MK.ap[s].rearrange("(c p) n -> p c n", p=128), r=(MK,), w=(mkt,))
                    S.dma("sp", mvt.ap, MV.ap[s].rearrange("(j p) f -> p j f", p=128), r=(MV,), w=(mvt,))
                if ti + 1 < NT:
                    pm2_load(ti + 1)
                xs = xs2[ti % 2]
                xsc = xsc2[ti % 2]

                layer_norm(xsc, 512, ONES_D, C_L1G, C_L1B, (banks[3], banks[4]), tmpm, out_bf=x1b)
                if ti + 1 < NT:
                    dense_res(0, mgb2[(ti + 1) % 2], xsc2[(ti + 1) % 2])
                for c in range(8):
                    bk = banks[pbi % 3]
                    pbi += 1
                    for k in range(8):
                        mm(bk.ap, W3[:, 1, k, c * 128:(c + 1) * 128], x1b.ap[:, k, :], k == 0, k == 7, r=(WB, x1b), w=(bk,))
                    cp("act" if c % 2 == 0 else "dve", qb.ap[:, c, :], bk.ap, r=(bk,), w=(qb,))
                for h in range(4):
                    acc_o = (banks[5], banks[6])
                    acc_s = banks[7]
                    for kb in range(2):
                        bk = banks[pbi % 3]
                        pbi += 1
                        for dc in range(2):
                            mm(bk.ap, mkt.ap[:, h * 2 + dc, kb * 128:(kb + 1) * 128], qb.ap[:, h * 2 + dc, :], dc == 0, dc == 1, r=(mkt, qb), w=(bk,))
                        p = pt[pi_ % 4]
                        pi_ += 1
                        act(p.ap, bk.ap, AF.Exp, r=(bk,), w=(p,), scale=1.0 / 16)
                        for ec in range(2):
                            mm(acc_o[ec].ap, mvt.ap[:, kb, (h * 2 + ec) * 128:(h * 2 + ec + 1) * 128], p.ap, kb == 0, kb == 1, r=(mvt, p), w=(acc_o[ec],))
                        mm(acc_s.ap, onesb.ap, p.ap, kb == 0, kb == 1, r=(onesb, p), w=(acc_s,))
                    recip(rcp, rcp.ap, acc_s.ap, acc_s)
                    for ec in range(2):
                        tt("dve", ob2.ap[:, h * 2 + ec, :], acc_o[ec].ap, rcp.ap, ALU.mult, r=(acc_o[ec], rcp), w=(ob2,))
                dense_res(2, ob2, xsc)
                layer_norm(xsc, 512, ONES_D, C_L2G, C_L2B, (banks[3], banks[4]), tmpm)
                S.dma("sp", X2.ap[:, g0:g0 + 512].rearrange("(k p) n -> p k n", p=128), xs.ap, r=tuple(xsc), w=(X2,))

            for hf in range(2):
                phase()
                AR = WA if hf == 0 else WB
                Wu = AR.ap[:, 0:8 * 2816].rearrange("p (k n) -> p k n", k=8)
                gc0 = hf * 11 * 128
                ng = min(11 * 128, FFN - gc0)
                for k in range(8):
                    load_w(AR, Wu[:, k, 0:ng], w_up[l, k * 128:(k + 1) * 128, gc0:gc0 + ng])
                    load_w(AR, Wu[:, k, 1408:1408 + ng], w_up[l, k * 128:(k + 1) * 128, FFN + gc0:FFN + gc0 + ng])
                xw2 = [A16.take([128, 8, 516]) for _ in range(2)]
                stf = [A16.take([128, 11, 512]) for _ in range(2)]
                hg = [A32.take([128, 516]) for _ in range(2)]
                hu = [A32.take([128, 516]) for _ in range(2)]
                cg = [A32.take([128, 512]) for _ in range(2)]
                cu = [A32.take([128, 512]) for _ in range(2)]
                gi = 0

                def pf_geom(ti):
                    s, l0, g0 = tiles[ti]
                    lo = max(l0 - 1, 0)
                    hi = min(l0 + 513, seqs[s])
                    return lo, hi - lo, lo - (l0 - 1)

                def pf_load(ti):
                    s, l0, g0 = tiles[ti]
                    lo, n, c0 = pf_geom(ti)
                    load_x_bf(xw2[ti % 2], X2, g0 - l0 + lo, n, col0=c0)

                pf_load(0)
                for ti, (s, l0, g0) in enumerate(tiles):
                    if ti + 1 < NT:
                        pf_load(ti + 1)
                    lo, n, c0 = pf_geom(ti)
                    xw = xw2[ti % 2]
                    n1 = min(n, 512)
                    n2 = n - n1
                    st = stf[ti % 2]
                    for ci in range(11):
                        M = min(128, ng - ci * 128)
                        fi = hf * 11 + ci
                        bg_, bu_ = banks[(gi % 2) * 2], banks[(gi % 2) * 2 + 1]
                        bg2, bu2 = banks[4 + (gi % 2) * 2], banks[5 + (gi % 2) * 2]
                        hgt, hut, cgt, cut = hg[gi % 2], hu[gi % 2], cg[gi % 2], cu[gi % 2]
                        gi += 1
                        for (bk, bk2, wc) in ((bg_, bg2, ci * 128), (bu_, bu2, 1408 + ci * 128)):
                            for k in range(8):
                                mm(bk.ap[0:M, 0:n1], Wu[:, k, wc:wc + M], xw.ap[:, k, c0:c0 + n1], k == 0, k == 7, r=(AR, xw), w=(bk,))
                            if n2 > 0:
                                for k in range(8):
                                    mm(bk2.ap[0:M, 0:n2], Wu[:, k, wc:wc + M], xw.ap[:, k, c0 + n1:c0 + n], k == 0, k == 7, r=(AR, xw), w=(bk2,))
                        for (ht, bk, bk2, eng) in ((hgt, bg_, bg2, "act"), (hut, bu_, bu2, "act")):
                            if c0 > 0:
                                memset("pool", ht, ht.ap[0:M, 0:c0], 0.0)
                            if c0 + n < 514:
                                memset("pool", ht, ht.ap[0:M, c0 + n:514], 0.0)
                            cp(eng, ht.ap[0:M, c0:c0 + n1], bk.ap[0:M, 0:n1], r=(bk,), w=(ht,))
                            if n2 > 0:
                                cp(eng, ht.ap[0:M, c0 + n1:c0 + n], bk2.ap[0:M, 0:n2], r=(bk2,), w=(ht,))
                        for (ht, ct, ii) in ((hgt, cgt, fi), (hut, cut, 22 + fi)):
                            act(ct.ap[0:M, :], ht.ap[0:M, 1:513], AF.Identity, r=(ht, ptab), w=(ct,), bias=pcol(C_FB + ii)[0:M], scale=pcol(C_FW + ii * 3 + 1)[0:M])
                            for j in (0, 2):
                                stt("dve", ct.ap[0:M, :], ht.ap[0:M, j:j + 512], pcol(C_FW + ii * 3 + j)[0:M], ct.ap[0:M, :], ALU.mult, ALU.add, r=(ht, ct, ptab), w=(ct,))
                        act(cgt.ap[0:M, :], cgt.ap[0:M, :], AF.Gelu, r=(cgt,), w=(cgt,))
                        tt("dve", st.ap[0:M, ci, :], cgt.ap[0:M, :], cut.ap[0:M, :], ALU.mult, r=(cgt, cut), w=(st,))
                        if M < 128:
                            memset("pool", st, st.ap[M:128, ci, :], 0.0)
                    S.dma("sp", ACS.ap[hf * 1408:(hf + 1) * 1408, g0:g0 + 512].rearrange("(c p) n -> p c n", p=128), st.ap, r=(st,), w=(ACS,))

            phase()
            Wd = WA.ap[:, 0:22 * 1024].rearrange("p (k n) -> p k n", k=22)
            for k in range(22):
                M = min(128, FFN - k * 128)
                load_w(WA, Wd[0:M, k, :], w_down[l, k * 128:k * 128 + M, :])
            xs2 = [A32.take([128, 8, 512]) for _ in range(2)]
            xsc2 = [chunks(t, 8) for t in xs2]
            tmpm = [A32.take([128, 512]) for _ in range(6)]
            at2 = [A16.take([128, 22, 512]) for _ in range(2)]
            pbi = 0

            def pf2_load(ti):
                g0 = tiles[ti][2]
                S.dma("sp", xs2[ti % 2].ap, X2.ap[:, g0:g0 + 512].rearrange("(k p) n -> p k n", p=128), r=(X2,), w=tuple(xsc2[ti % 2]))
                S.dma("sp", at2[ti % 2].ap, ACS.ap[:, g0:g0 + 512].rearrange("(k p) n -> p k n", p=128), r=(ACS,), w=(at2[ti % 2],))

            pf2_load(0)
            for ti, (s, l0, g0) in enumerate(tiles):
                if ti + 1 < NT:
                    pf2_load(ti + 1)
                xs = xs2[ti % 2]
                xsc = xsc2[ti % 2]
                at = at2[ti % 2]
                for c in range(8):
                    bk = banks[pbi % 3]
                    pbi += 1
                    for k in range(22):
                        M = min(128, FFN - k * 128)
                        mm(bk.ap, Wd[0:M, k, c * 128:(c + 1) * 128], at.ap[0:M, k, :], k == 0, k == 21, r=(WA, at), w=(bk,))
                    stt("dve", xsc[c].ap, xsc[c].ap, ALPHA, bk.ap, ALU.mult, ALU.add, r=(xsc[c], bk), w=(xsc[c],))
                layer_norm(xsc, 512, ONES_D, C_L3G, C_L3B, (banks[3], banks[4]), tmpm)
                S.dma("sp", XOUT.ap[:, g0:g0 + 512].rearrange("(k p) n -> p k n", p=128), xs.ap, r=tuple(xsc), w=(XOUT,))

        S.barrier()
        S.emit()
    return nc


def host_consts(smax):
    cm = np.zeros((128, 512), np.float32)
    for p in range(128):
        cm[p, 127 - p] = 1.0
        cm[p, 256 + p] = 1.0
    for d in range(128):
        dd = d % 32
        partner = d + 16 if dd < 16 else d - 16
        cm[partner, 128 + d] = 1.0
    t = np.arange(smax)
    row = (t // 64).astype(np.float32)
    col = (t % 64).astype(np.float32)
    inv = (np.float32(10000.0) ** (-np.arange(16, dtype=np.float32) / np.float32(16))).astype(np.float32)
    rope = np.zeros((2, 128, smax), np.float32)
    for p in range(128):
        d = p % 64
        pos = row if d < 32 else col
        dd = d % 32
        ang = (pos * inv[dd % 16]).astype(np.float32)
        rope[0, p] = np.cos(ang)
        rope[1, p] = -np.sin(ang) if dd < 16 else np.sin(ang)
    i = np.arange(TD_L)
    bd = t5_bucket_np((127 + 512 - i).astype(np.int32))
    ohd = np.zeros((33, TD_L), np.float32)
    ohd[bd, i] = 1.0
    i = np.arange(TW_L)
    relw = (255 - i).astype(np.int32)
    bw = t5_bucket_np(relw)
    ohw = np.zeros((33, TW_L), np.float32)
    ohw[bw, i] = 1.0
    ohw[32, :] = np.where(np.abs(relw) > 128, NEGM, 0.0)
    return cm, rope, ohd, ohw


def host_ptab(inp):
    L = DEPTH
    pt = np.zeros((L, 128, NCOL), np.float32)
    f32 = lambda a: np.asarray(a, dtype=np.float32)
    b_in = f32(inp["b_in"])
    pt[:, :, C_BIN:C_BIN + 64] = b_in.reshape(L, 64, 128).transpose(0, 2, 1)
    cw = f32(inp["a_conv_w"])
    pt[:, :, C_CW:C_CW + 124] = cw.reshape(L, 31, 4, 128).transpose(0, 3, 2, 1).reshape(L, 128, 124)
    for name, c in (("a_conv_b", C_CB), ("a_ln_g", C_ALG), ("a_ln_b", C_ALB)):
        pt[:, :, c:c + 4] = f32(inp[name]).reshape(L, 4, 128).transpose(0, 2, 1)
    for name, c in (("ln1_g", C_L1G), ("ln1_b", C_L1B), ("ln2_g", C_L2G), ("ln2_b", C_L2B), ("ln3_g", C_L3G), ("ln3_b", C_L3B)):
        pt[:, :, c:c + 8] = f32(inp[name]).reshape(L, 8, 128).transpose(0, 2, 1)
    fw = f32(inp["f_conv_w"])
    fb = f32(inp["f_conv_b"])
    for i in range(44):
        f0 = (i * 128) if i < 22 else (FFN + (i - 22) * 128)
        M = min(128, FFN - (i % 22) * 128)
        pt[:, 0:M, C_FW + i * 3:C_FW + i * 3 + 3] = fw[:, :, f0:f0 + M].transpose(0, 2, 1)
        pt[:, 0:M, C_FB + i] = fb[:, f0:f0 + M]
    pt[:, :, C_SUBG] = f32(inp["diff_sub_g"])
    pt[:, :, C_QNG] = np.tile(f32(inp["ax_qn_g"]), (1, 2))
    pt[:, :, C_KNG] = np.tile(f32(inp["ax_kn_g"]), (1, 2))
    brow = np.concatenate([b_in[:, O_BV:O_BV + 512], b_in[:, O_CV:O_CV + 128], b_in[:, O_DV:O_DV + 128]], axis=1)
    brow = np.ascontiguousarray(np.broadcast_to(brow[:, None, :], (L, 128, 768)))
    lam = f32(inp["diff_lam"]).reshape(L, 1, 256)
    sink = np.ascontiguousarray(np.broadcast_to(f32(inp["win_sink"])[:, None, :], (L, 128, 8)))
    return pt, brow, lam, sink


def make_in_maps(inp, core_seqs):
    smax = max(x.shape[0] for cs in core_seqs for (x, m) in cs)
    cm, rope, ohd, ohw = host_consts(smax)
    pt, brow, lam, sink = host_ptab(inp)
    shared = {k: np.ascontiguousarray(np.asarray(inp[k], dtype=np.float32)) for k in
              ("rel_bias", "w_in", "w_branch", "w_mix_out", "w_xq", "w_xkv", "w_xo", "w_up", "w_down")}
    shared.update(ptab=pt, brow=brow, lam=lam, sink=sink, cmat=cm, rope=rope, ohd=ohd, ohw=ohw)
    maps = []
    for cs in core_seqs:
        xT = np.ascontiguousarray(np.concatenate([np.asarray(x, np.float32).T for (x, m) in cs], axis=1))
        mT = np.ascontiguousarray(np.concatenate([np.asarray(m, np.float32).T for (x, m) in cs], axis=1))
        d = dict(shared)
        d["xT"] = xT
        d["memT"] = mT
        maps.append(d)
    return maps


def kernel(**inp):
    xp = np.asarray(inp["x_prompt"], np.float32)
    xs = np.asarray(inp["x_sample"], np.float32)
    mp = np.asarray(inp["mem_prompt"], np.float32)
    ms = np.asarray(inp["mem_sample"], np.float32)
    ncore = 8
    core_seqs = []
    for c in range(ncore):
        core_seqs.append([(xp[2 * c], mp[2 * c]), (xp[2 * c + 1], mp[2 * c + 1]),
                          (xs[2 * c], ms[2 * c]), (xs[2 * c + 1], ms[2 * c + 1])])
    seqs = [x.shape[0] for (x, m) in core_seqs[0]]
    nc = build(seqs)
    maps = make_in_maps(inp, core_seqs)
    res = run_bass_kernel_spmd(nc, maps, core_ids=list(range(ncore)))
    yp = np.empty_like(xp)
    ys = np.empty_like(xs)
    for c in range(ncore):
        yT = res.results[c]["yT"]
        o = 0
        for i, (x, m) in enumerate(core_seqs[c]):
            Sq = x.shape[0]
            blk = yT[:, o:o + Sq].T
            o += Sq
            if i < 2:
                yp[2 * c + i] = blk
            else:
                ys[2 * c + (i - 2)] = blk
    return (yp, ys)
```
